# Optimizing a Trainium2 kernel written in Bass

```python
import jax
import jax.numpy as jnp
from jax import lax
import numpy as np

D_MODEL = 2048
BATCH = 2
SEQ = 16384
DEPTH = 2
DEC_BATCH = 16
DEC_SEQ = 16
PAST_LEN = 4096

CHUNK = 64
N_BRANCH = 4
D_A = D_MODEL // 4
D_B = D_MODEL // 4
D_C = D_MODEL // 4
D_D = D_MODEL // 4
CONV_A_WIDTH = 31
CONV_B_WIDTH = 3
SPATIAL_CHUNK = 128
C_GROUPS = 4
C_GROUP_W = D_C // C_GROUPS
POOL_WINDOWS = (2, 4, 8, 16)
POOL_GROUP_W = D_D // len(POOL_WINDOWS)
POOL_HIST = max(POOL_WINDOWS) - 1
D_FF = -(-8 * D_MODEL // (3 * 256)) * 256

COL_A = 2 * D_A
COL_B = 3 * D_B
COL_C = 2 * D_C
COL_D = D_D
COL_G = N_BRANCH * D_MODEL
IN_COLS = COL_A + COL_B + COL_C + COL_D + COL_G
SPLIT_IDX = (COL_A, COL_A + COL_B, COL_A + COL_B + COL_C, COL_A + COL_B + COL_C + COL_D)

RMS_EPS = 1e-6
LN_EPS = 1e-5

kernel_name = "gated_parallel_conv_pool_gmlp_stream_step"


def rmsnorm(x, g):
    xf = x.astype(jnp.float32)
    y = xf * lax.rsqrt(jnp.mean(xf * xf, axis=-1, keepdims=True) + RMS_EPS)
    return (y * g.astype(jnp.float32)).astype(x.dtype)


def layernorm(x, g, b):
    xf = x.astype(jnp.float32)
    mu = jnp.mean(xf, axis=-1, keepdims=True)
    var = jnp.mean(jnp.square(xf - mu), axis=-1, keepdims=True)
    y = (xf - mu) * lax.rsqrt(var + LN_EPS)
    return (y * g.astype(jnp.float32) + b.astype(jnp.float32)).astype(x.dtype)


def causal_dwconv(x, hist, w):
    k = w.shape[0]
    xp = jnp.concatenate([hist.astype(x.dtype), x], axis=1)
    y = lax.conv_general_dilated(xp, w[:, None, :].astype(x.dtype), window_strides=(1,), padding='VALID',
                                 dimension_numbers=('NWC', 'WIO', 'NWC'), feature_group_count=x.shape[-1])
    return y, xp[:, xp.shape[1] - (k - 1):]


def multiscale_pool(x, hist, start):
    t = x.shape[1]
    xp = jnp.concatenate([hist.astype(x.dtype), x], axis=1)
    cs = jnp.pad(jnp.cumsum(xp.astype(jnp.float32), axis=1), ((0, 0), (1, 0), (0, 0)))
    n_avail = start + jnp.arange(t, dtype=jnp.int32) + 1
    parts = []
    for g, w in enumerate(POOL_WINDOWS):
        lo, hi = g * POOL_GROUP_W, (g + 1) * POOL_GROUP_W
        s_end = cs[:, POOL_HIST + 1:POOL_HIST + 1 + t, lo:hi]
        s_beg = cs[:, POOL_HIST + 1 - w:POOL_HIST + 1 - w + t, lo:hi]
        cnt = jnp.minimum(n_avail, w).astype(jnp.float32)[None, :, None]
        parts.append((s_end - s_beg) / cnt)
    pooled = jnp.concatenate(parts, axis=-1) - x.astype(jnp.float32)
    return pooled.astype(x.dtype), xp[:, xp.shape[1] - POOL_HIST:]


def spatial_gating(u, v, w_s, b_s):
    nb, t, _ = v.shape
    n = -(-t // SPATIAL_CHUNK)
    vp = jnp.pad(v, ((0, 0), (0, n * SPATIAL_CHUNK - t), (0, 0)))
    vp = vp.reshape(nb, n, SPATIAL_CHUNK, C_GROUPS, C_GROUP_W)
    mask = jnp.tril(jnp.ones((SPATIAL_CHUNK, SPATIAL_CHUNK), dtype=bool))
    wm = jnp.where(mask[None], w_s, jnp.zeros((), w_s.dtype))
    mixed = jnp.einsum('gts,bnsgc->bntgc', wm, vp) + b_s.T[None, None, :, :, None]
    mixed = mixed.reshape(nb, n * SPATIAL_CHUNK, D_C)[:, :t]
    return u * mixed


def trunk_layer(x, hist_a, hist_b, hist_d, start, norm_mix_g, w_in, conv_a_w, conv_a_b, ln_a_g, ln_a_b,
                w_out_a, conv_b_w, w_out_b, ln_c_g, ln_c_b, spatial_w, spatial_b, w_out_c, pool_w,
                pool_scale, w_out_d, w_o, norm_ffn_g, ffn_w1, ffn_w3, ffn_w2):
    nb, t, _ = x.shape
    h = rmsnorm(x, norm_mix_g)
    z = h @ w_in
    za, zb, zc, zd, zg = jnp.split(z, SPLIT_IDX, axis=-1)

    a = za[..., :D_A] * jax.nn.sigmoid(za[..., D_A:])
    a_conv, new_a = causal_dwconv(a, hist_a, conv_a_w)
    a_out = jax.nn.silu(layernorm(a_conv + conv_a_b, ln_a_g, ln_a_b)) @ w_out_a

    bg, cg, hb = jnp.split(zb, 3, axis=-1)
    b_conv, new_b = causal_dwconv(cg * hb, hist_b, conv_b_w)
    b_out = (bg * b_conv) @ w_out_b

    zc = jax.nn.gelu(zc, approximate=False)
    u, v = jnp.split(zc, 2, axis=-1)
    v = layernorm(v, ln_c_g, ln_c_b)
    c_out = spatial_gating(u, v, spatial_w, spatial_b) @ w_out_c
    v_rows = v[:, ((t - 1) // SPATIAL_CHUNK) * SPATIAL_CHUNK:]

    d_pool, new_d = multiscale_pool(zd, hist_d, start)
    d_mix = jnp.einsum('btgc,gcd->btgd', d_pool.reshape(nb, t, len(POOL_WINDOWS), POOL_GROUP_W), pool_w)
    d_out = (d_mix.reshape(nb, t, D_D) * pool_scale) @ w_out_d

    gate = jax.nn.sigmoid(zg)
    merged = (gate[..., 0 * D_MODEL:1 * D_MODEL] * a_out + gate[..., 1 * D_MODEL:2 * D_MODEL] * b_out
              + gate[..., 2 * D_MODEL:3 * D_MODEL] * c_out + gate[..., 3 * D_MODEL:4 * D_MODEL] * d_out)
    x = x + merged @ w_o

    h2 = rmsnorm(x, norm_ffn_g)
    x = x + (jax.nn.silu(h2 @ ffn_w1) * (h2 @ ffn_w3)) @ ffn_w2
    return x, new_a, new_b, new_d, v_rows


def setup_inputs(seed: int = 0) -> dict:
    key = jax.random.key(seed)
    ks = iter(jax.random.split(key, 32))

    def nrm(shape, scale):
        return jax.random.normal(next(ks), shape, jnp.float32) * scale

    L = DEPTH
    return {
        'x_prompt': nrm((BATCH, SEQ, D_MODEL), 1.0),
        'x_sample': nrm((DEC_BATCH, DEC_SEQ, D_MODEL), 1.0),
        'state_conv_a': nrm((L, DEC_BATCH, CONV_A_WIDTH - 1, D_A), 0.5),
        'state_conv_b': nrm((L, DEC_BATCH, CONV_B_WIDTH - 1, D_B), 0.5),
        'state_pool': nrm((L, DEC_BATCH, POOL_HIST, D_D), 1.0),
        'norm_mix_g': 1.0 + nrm((L, D_MODEL), 0.05),
        'w_in': nrm((L, D_MODEL, IN_COLS), D_MODEL ** -0.5),
        'conv_a_w': nrm((L, CONV_A_WIDTH, D_A), CONV_A_WIDTH ** -0.5),
        'conv_a_b': nrm((L, D_A), 0.02),
        'ln_a_g': 1.0 + nrm((L, D_A), 0.05),
        'ln_a_b': nrm((L, D_A), 0.02),
        'w_out_a': nrm((L, D_A, D_MODEL), D_A ** -0.5),
        'conv_b_w': nrm((L, CONV_B_WIDTH, D_B), CONV_B_WIDTH ** -0.5),
        'w_out_b': nrm((L, D_B, D_MODEL), D_B ** -0.5),
        'ln_c_g': 1.0 + nrm((L, D_C), 0.05),
        'ln_c_b': nrm((L, D_C), 0.02),
        'spatial_w': nrm((L, C_GROUPS, SPATIAL_CHUNK, SPATIAL_CHUNK), SPATIAL_CHUNK ** -0.5),
        'spatial_b': 1.0 + nrm((L, C_GROUPS, SPATIAL_CHUNK), 0.05),
        'w_out_c': nrm((L, D_C, D_MODEL), D_C ** -0.5),
        'pool_w': nrm((L, len(POOL_WINDOWS), POOL_GROUP_W, POOL_GROUP_W), POOL_GROUP_W ** -0.5),
        'pool_scale': 1.0 + nrm((L, D_D), 0.1),
        'w_out_d': nrm((L, D_D, D_MODEL), D_D ** -0.5),
        'w_o': nrm((L, D_MODEL, D_MODEL), D_MODEL ** -0.5),
        'norm_ffn_g': 1.0 + nrm((L, D_MODEL), 0.05),
        'ffn_w1': nrm((L, D_MODEL, D_FF), D_MODEL ** -0.5),
        'ffn_w3': nrm((L, D_MODEL, D_FF), D_MODEL ** -0.5),
        'ffn_w2': nrm((L, D_FF, D_MODEL), D_FF ** -0.5),
        'norm_final_g': 1.0 + nrm((D_MODEL,), 0.05),
    }


def reference(x_prompt, x_sample, state_conv_a, state_conv_b, state_pool, norm_mix_g, w_in, conv_a_w,
              conv_a_b, ln_a_g, ln_a_b, w_out_a, conv_b_w, w_out_b, ln_c_g, ln_c_b, spatial_w, spatial_b,
              w_out_c, pool_w, pool_scale, w_out_d, w_o, norm_ffn_g, ffn_w1, ffn_w3, ffn_w2, norm_final_g):
    def run_group(x, hist_a, hist_b, hist_d, start):
        new_a, new_b, new_d, new_v = [], [], [], []
        for l in range(DEPTH):
            x, na, nbuf, nd, vr = trunk_layer(
                x, hist_a[l], hist_b[l], hist_d[l], start, norm_mix_g[l], w_in[l], conv_a_w[l], conv_a_b[l],
                ln_a_g[l], ln_a_b[l], w_out_a[l], conv_b_w[l], w_out_b[l], ln_c_g[l], ln_c_b[l], spatial_w[l],
                spatial_b[l], w_out_c[l], pool_w[l], pool_scale[l], w_out_d[l], w_o[l], norm_ffn_g[l],
                ffn_w1[l], ffn_w3[l], ffn_w2[l])
            new_a.append(na)
            new_b.append(nbuf)
            new_d.append(nd)
            new_v.append(vr)
        return (rmsnorm(x, norm_final_g), jnp.stack(new_a), jnp.stack(new_b), jnp.stack(new_d), jnp.stack(new_v))

    dt = x_prompt.dtype
    zero_a = jnp.zeros((DEPTH, BATCH, CONV_A_WIDTH - 1, D_A), dt)
    zero_b = jnp.zeros((DEPTH, BATCH, CONV_B_WIDTH - 1, D_B), dt)
    zero_d = jnp.zeros((DEPTH, BATCH, POOL_HIST, D_D), dt)
    y_prompt, conv_a_prompt, conv_b_prompt, pool_prompt, vrows_prompt = run_group(
        x_prompt, zero_a, zero_b, zero_d, 0)
    y_sample, conv_a_sample, conv_b_sample, pool_sample, vrows_sample = run_group(
        x_sample, state_conv_a, state_conv_b, state_pool, PAST_LEN)
    return (y_prompt, y_sample, conv_a_prompt, conv_a_sample, conv_b_prompt, conv_b_sample,
            pool_prompt, pool_sample, vrows_prompt, vrows_sample)
```

```python
import numpy as np
import concourse.bass as bass
import concourse.mybir as mybir
from concourse.bass_utils import run_bass_kernel_spmd

F32 = mybir.dt.float32
BF16 = mybir.dt.bfloat16
U8 = mybir.dt.uint8
AF = mybir.ActivationFunctionType
ALU = mybir.AluOpType

L = 2
D = 2048
KC = 16
DFF = 5632
FC = 44
NCORE = 8
SEG = 4096
HALO = 128
NS = 32
T0 = HALO + NS
NT = 512
NTOK_ = T0 + SEG
PEND = HALO + SEG
TILES = [(i * NT, NT) for i in range(8)] + [(8 * NT, NTOK_ - 8 * NT)]
NTILE = len(TILES)
NTOK = T0 + SEG
HA, HB, HD = 30, 2, 15
RMS_EPS = 1e-6
LN_EPS = 1e-5
NG = 70
NSLOT = 3
SBCELL = 256
PSCELL = 512


class Ins:
    __slots__ = ("eng", "fn", "reads", "writes", "dma", "deps", "signal", "count", "dsem", "dval", "dprev", "idx")

    def __init__(self, eng, fn, reads, writes, dma):
        self.eng, self.fn, self.reads, self.writes, self.dma = eng, fn, reads, writes, dma
        self.deps = []
        self.signal = False
        self.count = 0
        self.dsem = None
        self.dval = 0
        self.dprev = 0


class Prog:
    def __init__(self):
        self.ins = []
        self.lastw = {}
        self.readers = {}
        self.stopped = False

    def add(self, eng, fn, reads=(), writes=(), dma=False, dsem=None):
        I = Ins(eng, fn, reads, writes, dma)
        if self.stopped:
            return I
        I.dsem = dsem
        I.idx = len(self.ins)
        cdeps = {}
        ddeps = {}
        lastw, readers = self.lastw, self.readers

        def dep(w):
            if w is I:
                return
            if w.dma:
                ddeps[w.idx] = w
            else:
                c = cdeps.get(w.eng)
                if c is None or c.idx < w.idx:
                    cdeps[w.eng] = w
        for k in reads:
            w = lastw.get(k)
            if w is not None:
                dep(w)
        for k in writes:
            w = lastw.get(k)
            if w is not None and (w.dma or dma or w.eng != eng):
                dep(w)
            rs = readers.get(k)
            if rs:
                for r in rs.values():
                    if r.dma or dma or r.eng != eng:
                        dep(r)
        for k in reads:
            rd = readers.get(k)
            if rd is None:
                rd = readers[k] = {}
            rd[("dma", I.idx) if dma else eng] = I
        for k in writes:
            lastw[k] = I
            readers[k] = {}
        I.deps = list(cdeps.values()) + list(ddeps.values())
        for d in cdeps.values():
            d.signal = True
        self.ins.append(I)
        return I


def sbc(off, nbytes):
    return [("s", c) for c in range(off // SBCELL, (off + nbytes - 1) // SBCELL + 1)]


class Buf:
    def __init__(self, arena, off, shape, dt):
        self.off = off
        self.shape = shape
        self.dt = dt
        self.esz = 4 if dt == F32 else 2
        n = int(np.prod(shape))
        self.nbytes = n * self.esz
        ap = arena[:, off:off + self.nbytes].bitcast(dt)
        if len(shape) == 2:
            ap = ap.rearrange("p (a b) -> p a b", a=shape[0])
        elif len(shape) == 3:
            ap = ap.rearrange("p (a b c) -> p a b c", a=shape[0], b=shape[1])
        self.ap = ap
        self.row = shape[-1] * self.esz if len(shape) > 1 else self.nbytes

    def cells(self, i=None, lo=None, hi=None):
        if i is None:
            return sbc(self.off, self.nbytes)
        base = self.off + i * self.row
        if lo is None:
            return sbc(base, self.row)
        return sbc(base + lo * self.esz, (hi - lo) * self.esz)


def psc(bank, lo=0, hi=512):
    b0 = bank * 2048 + lo * 4
    b1 = bank * 2048 + hi * 4 - 1
    return [("p", c) for c in range(b0 // PSCELL, b1 // PSCELL + 1)]


def pcol_layout():
    off = {}
    n = 0
    for l in range(L):
        for name, w in (("gmix", 16), ("gffn", 16), ("caw", 4 * 31), ("cab", 4), ("lag", 4), ("lab", 4),
                        ("cbw", 4 * 3), ("psc", 4)):
            off[(name, l)] = n
            n += w
    off["gfin"] = n
    n += 16
    off["flag"] = n
    n += 1
    return off, n


PCO, NPC = pcol_layout()


def group_index():
    gi = {}
    n = 0
    for j in range(8):
        gi[("in", j)] = n; n += 1
    for mg in range(4):
        for b in range(4):
            gi[("gate", b, mg)] = n; n += 1
        gi[("wo", mg)] = n; n += 1
    for og in range(4):
        gi[("o", og)] = n; n += 1
    for fg in range(11):
        gi[("w1", fg)] = n; n += 1
        gi[("w3", fg)] = n; n += 1
    for og in range(4):
        for kg in range(4):
            gi[("w2", og, kg)] = n; n += 1
    assert n == NG
    return gi


GI = group_index()


def build_nc():
    nc = bass.Bass("TRN2", target_bir_lowering=False)
    dt_in = lambda name, shape: nc.dram_tensor(name, shape, F32, kind="ExternalInput").ap()
    dt_out = lambda name, shape: nc.dram_tensor(name, shape, F32, kind="ExternalOutput").ap()
    xin = dt_in("xin", [128, KC, NTOK])
    w_in = dt_in("w_in", [L, D, 12288])
    w_oa = [dt_in("w_out_" + c, [L, 512, D]) for c in "abcd"]
    w_o = dt_in("w_o", [L, D, D])
    w1 = dt_in("ffn_w1", [L, D, DFF])
    w3 = dt_in("ffn_w3", [L, D, DFF])
    w2 = dt_in("ffn_w2", [L, DFF, D])
    pcols_d = dt_in("pcols", [128, NPC])
    pbc_d = dt_in("pbc", [L, 128, 2, 512])
    pbs_d = dt_in("pbs", [L, 128, 4, 128])
    pbsS_d = dt_in("pbsS", [L, 128, 4, 32])
    wsT_d = dt_in("wsT", [128, L, 4, 128])
    wsS_d = dt_in("wsS", [32, L, 4, 32])
    maskP_d = dt_in("maskP", [128, 128])
    maskS_d = dt_in("maskS", [32, 32])
    ident_d = dt_in("ident", [128, 128])
    pw_d = dt_in("pw", [128, L, 4, 128])
    pcorr_d = dt_in("pcorr", [128, 4, 16])
    sta_d = dt_in("sta", [L, 2, 128, 4, HA])
    stb_d = dt_in("stb", [L, 2, 128, 4, HB])
    std_d = dt_in("std", [L, 2, 128, 4, HD])
    yT = dt_out("yT", [128, KC, NTOK])
    oa = dt_out("oa", [L, 128, 4, HA])
    ob = dt_out("ob", [L, 128, 4, HB])
    od = dt_out("od", [L, 128, 4, HD])
    oas = dt_out("oas", [L, 2, 128, 4, HA])
    obs = dt_out("obs", [L, 2, 128, 4, HB])
    ods = dt_out("ods", [L, 2, 128, 4, HD])
    ov = dt_out("ov", [L, 128, 512])
    ovs = dt_out("ovs", [L, NS, 512])
    wsc_l = [nc.dram_tensor("wsc%d" % l, [NG, 128, 16, 512], BF16, kind="Internal").ap() for l in range(L)]

    class _W:
        def __getitem__(self, g):
            return wsc_l[g // NG][g % NG]
    wsc = _W()

    P = Prog()
    STOP = ""

    def chk(tag):
        if STOP and STOP == tag:
            P.stopped = True
    from contextlib import ExitStack
    es = ExitStack()
    ARENA_BYTES = 211200
    arena = es.enter_context(nc.sbuf_tensor("arena", [128, ARENA_BYTES], U8))
    psum = es.enter_context(nc.psum_tensor("psum", [128, 8, 512], F32))
    sem_eng = {e: es.enter_context(nc.semaphore("sem_" + e)) for e in ("pe", "act", "dve", "pool")}
    sem_slot = [es.enter_context(nc.semaphore("sem_slot%d" % i)) for i in range(NSLOT)]
    sem_cv = [es.enter_context(nc.semaphore("sem_cv%d" % i)) for i in range(6)]
    sem_io = [es.enter_context(nc.semaphore("sem_io%d" % i)) for i in range(8)]
    sem_st = [es.enter_context(nc.semaphore("sem_st%d" % i)) for i in range(4)]

    cur = [0]

    def alloc(shape, dt, at=None):
        esz = 4 if dt == F32 else 2
        nb = int(np.prod(shape)) * esz
        if at is None:
            off = cur[0]
            cur[0] = (off + nb + SBCELL - 1) // SBCELL * SBCELL
        else:
            off = at
        return Buf(arena, off, shape, dt)

    xT = alloc([KC, NT], F32)
    h = alloc([KC, NT], BF16)
    wbuf = [alloc([16, 512], BF16) for _ in range(NSLOT)]
    pcols = alloc([NPC], F32)
    pbc = alloc([2, 512], F32)
    pbs = alloc([4, 128], F32)
    pbsS = alloc([4, 32], F32)
    WmT = alloc([L, 4, 128], BF16)
    WmS = alloc([L, 4, 32], BF16)
    pwb = alloc([L, 4, 128], BF16)
    onesD = alloc([128], BF16)
    onesA = alloc([128], BF16)
    epsR = alloc([1], F32)
    epsL = alloc([1], F32)
    pcorr = alloc([4, 16], F32)
    hAb = alloc([L, 4, HA], F32)
    hBb = alloc([L, 4, HB], F32)
    hDb = alloc([L, 4, HD], F32)
    st6 = alloc([4, 6], F32)
    mv = alloc([2], F32)
    R0 = cur[0]
    br = alloc([4, 4, NT], BF16)
    EWA, EWB, EWD = HA + NT, HB + NT, HD + NT
    a_ext = alloc([4, EWA], F32)
    cgh = alloc([4, EWB], F32)
    zd = alloc([4, EWD], F32)
    u = alloc([4, NT], BF16)
    cacc = alloc([4, 528], F32)
    bconv = alloc([4, 528], F32)
    cbf = alloc([4, NT], BF16)
    sq = alloc([4, NT], BF16)
    mean = alloc([NT], F32)
    ex2 = alloc([NT], F32)
    rstdA = alloc([NT], F32)
    vnbf = alloc([4, 512], BF16)
    sg = alloc([12, NT], BF16)
    dring = alloc([8, 128], BF16)
    identb = alloc([128], F32)
    REND = cur[0]
    assert REND <= ARENA_BYTES, REND
    sgA = alloc([4, NT], F32, at=cacc.off)
    a_bf = alloc([4, EWA], BF16, at=cbf.off)
    cgt = alloc([4, NT], F32, at=cbf.off)
    pooled = alloc([4, NT], BF16, at=cbf.off)
    vt = [alloc([512], F32, at=mean.off), alloc([512], F32, at=rstdA.off)]
    sg3 = alloc([4, NT], BF16, at=u.off)
    mgb = alloc([KC, NT], BF16, at=a_ext.off)
    assert mgb.off + mgb.nbytes <= u.off
    macc = alloc([4, NT], F32, at=bconv.off)
    mtmp = alloc([2, NT], F32, at=cbf.off)
    hid = alloc([FC, NT], BF16, at=R0)
    assert hid.off + hid.nbytes <= cacc.off
    ftmp = alloc([2, NT], F32, at=cacc.off)
    sqb = alloc([4, NT], BF16, at=cacc.off + 4096)
    ystg = alloc([2, 4, NT], F32, at=cacc.off + 8192)
    assert ystg.off + ystg.nbytes <= mean.off
    rs = alloc([NT], F32, at=ex2.off)
    wsTf = alloc([L, 4, 128], F32, at=R0)
    wsSf = alloc([L, 4, 32], F32, at=R0 + 4096)
    mP = alloc([128], F32, at=R0 + 8192)
    mS = alloc([32], F32, at=R0 + 8192 + 512)

    ps = lambda b: psum[:, b, :]

    io_rr = [0]
    cv_rr = [0]
    gres = {}

    def pe(fn, r, w):
        return P.add("pe", fn, r, w)

    def act(fn, r, w):
        return P.add("act", fn, r, w)

    def dve(fn, r, w):
        return P.add("dve", fn, r, w)

    def pool(fn, r, w):
        return P.add("pool", fn, r, w)

    def dma_io(out, in_, r, w):
        s = sem_io[io_rr[0] % len(sem_io)]
        io_rr[0] += 1
        return P.add("act", lambda e, o=out, i=in_: e.dma_start(out=o, in_=i), r, w, dma=True, dsem=s)

    def dma_cv(out, in_, r, w, split=True):
        nk = out.shape[1] if split else 0
        pieces = [(out, in_)] if (not split or nk <= 4) else [(out[:, a:min(a + 4, nk)], in_[:, a:min(a + 4, nk)]) for a in range(0, nk, 4)]
        last = None
        for (o_, i_) in pieces:
            s = sem_cv[cv_rr[0] % len(sem_cv)]
            cv_rr[0] += 1
            w_ = list(w)
            if w and w[0][0] == "d":
                res = ("d", w[0][1], cv_rr[0])
                gres.setdefault(w[0][1], []).append(res)
                w_ = [res]
            last = P.add("pool", lambda e, o=o_, i=i_: e.dma_start(out=o, in_=i), r, w_, dma=True, dsem=s)
        return last

    slot_rr = [0]

    converted = set()
    st_rr = [0]

    def load_group(g):
        s = slot_rr[0] % NSLOT
        slot_rr[0] += 1
        if g not in converted:
            converted.add(g)
            for (dlo, srcv) in gsrc[g]:
                nk = srcv.shape[1]
                for a_ in range(0, nk, 4):
                    b_ = min(a_ + 4, nk)
                    sm = sem_cv[cv_rr[0] % len(sem_cv)]
                    cv_rr[0] += 1
                    P.add("pool", lambda e, o=wbuf[s].ap[:, dlo + a_:dlo + b_, :], i=srcv[:, a_:b_]: e.dma_start(out=o, in_=i),
                          [], [c for kc in range(dlo + a_, dlo + b_) for c in wbuf[s].cells(kc)], dma=True, dsem=sm)
            sm = sem_st[st_rr[0] % len(sem_st)]
            st_rr[0] += 1
            P.add("sp", lambda e, o=wsc[g], i=wbuf[s].ap: e.dma_start(out=o, in_=i),
                  wbuf[s].cells(), [("d", g)], dma=True, dsem=sm)
        else:
            P.add("sp", lambda e, o=wbuf[s].ap, i=wsc[g]: e.dma_start(out=o, in_=i),
                  [("d", g)], wbuf[s].cells(), dma=True, dsem=sem_slot[s])
        return s

    def pc(name, l=None, j=0):
        o = PCO[(name, l)] if l is not None else PCO[name]
        return pcols.ap[:, o + j:o + j + 1]

    dma_io(pcols.ap, pcols_d, [], pcols.cells())
    dma_io(pcorr.ap, pcorr_d, [], pcorr.cells())
    dma_io(wsTf.ap, wsT_d, [], wsTf.cells())
    dma_io(wsSf.ap[0:32], wsS_d, [], wsSf.cells())
    dma_io(mP.ap, maskP_d, [], mP.cells())
    dma_io(mS.ap[0:32], maskS_d, [], mS.cells())
    dma_cv(pwb.ap, pw_d, [], pwb.cells(), split=False)
    dma_cv(identb.ap, ident_d, [], identb.cells(), split=False)
    chk('init')
    dve(lambda e: e.memset(onesD.ap, 1.0 / D), [], onesD.cells())
    dve(lambda e: e.memset(onesA.ap, 1.0 / 512), [], onesA.cells())
    dve(lambda e: e.memset(epsR.ap, RMS_EPS), [], epsR.cells())
    dve(lambda e: e.memset(epsL.ap, LN_EPS), [], epsL.cells())
    for l in range(L):
        for g in range(4):
            dve(lambda e, l=l, g=g: e.tensor_tensor(out=WmT.ap[:, l, g, :], in0=wsTf.ap[:, l, g, :], in1=mP.ap, op=ALU.mult),
                wsTf.cells() + mP.cells(), WmT.cells())
            dve(lambda e, l=l, g=g: e.tensor_tensor(out=WmS.ap[0:32, l, g, :], in0=wsSf.ap[0:32, l, g, :], in1=mS.ap[0:32], op=ALU.mult),
                wsSf.cells() + mS.cells(), WmS.cells())

    def kview(ap, kc):
        return ap.rearrange("(kc p) n -> p kc n", p=128)

    gsrc = {}
    for l in range(L):
        base = l * NG
        for j in range(8):
            g = base + GI[("in", j)]
            gsrc.setdefault(g, []).append((0, kview(w_in[l][:, j * 512:(j + 1) * 512], 16)))
        for mg in range(4):
            for b in range(4):
                g = base + GI[("gate", b, mg)]
                c0 = 4096 + b * 2048 + mg * 512
                gsrc.setdefault(g, []).append((0, kview(w_in[l][:, c0:c0 + 512], 16)))
            g = base + GI[("wo", mg)]
            for b in range(4):
                gsrc.setdefault(g, []).append((b * 4, kview(w_oa[b][l][:, mg * 512:(mg + 1) * 512], 4)))
        for og in range(4):
            g = base + GI[("o", og)]
            gsrc.setdefault(g, []).append((0, kview(w_o[l][:, og * 512:(og + 1) * 512], 16)))
        for fg in range(11):
            g = base + GI[("w1", fg)]
            gsrc.setdefault(g, []).append((0, kview(w1[l][:, fg * 512:(fg + 1) * 512], 16)))
            g = base + GI[("w3", fg)]
            gsrc.setdefault(g, []).append((0, kview(w3[l][:, fg * 512:(fg + 1) * 512], 16)))
        for og in range(4):
            for kg in range(4):
                g = base + GI[("w2", og, kg)]
                gsrc.setdefault(g, []).append((0, kview(w2[l][kg * 1408:(kg + 1) * 1408, og * 512:(og + 1) * 512], 11)))

    chk('pre')
    bank_main = [0]
    bank_sec = [0]
    wo_rr = [0]
    dr_rr = [0]

    def nb_main():
        b = bank_main[0] % 4
        bank_main[0] += 1
        return b

    def nb_sec():
        b = 4 + bank_sec[0] % 2
        bank_sec[0] += 1
        return b

    def mm_group(bank, N, lhs_list, rhs_list, lhs_cells, rhs_cells, M=128):
        n = len(lhs_list)
        for k in range(n):
            pe(lambda e, k=k, o=psum[0:M, bank, 0:N], a=lhs_list[k], b=rhs_list[k], st=(k == 0), sp_=(k == n - 1):
               e.matmul(o, lhsT=a, rhs=b, start=st, stop=sp_),
               lhs_cells[k] + rhs_cells[k], psc(bank, 0, N))

    def sq_stat(N, kc):
        act(lambda e: e.activation(out=sqb.ap[:, kc % 4, 0:N], in_=xT.ap[:, kc, 0:N], func=AF.Square),
            xT.cells(kc, 0, N), sqb.cells(kc % 4, 0, N))
        pe(lambda e: e.matmul(psum[:, 6, 0:N], lhsT=onesD.ap, rhs=sqb.ap[:, kc % 4, 0:N], start=(kc == 0), stop=(kc == KC - 1)),
           onesD.cells() + sqb.cells(kc % 4, 0, N), psc(6, 0, N))

    def rstd_finish(N):
        act(lambda e: e.activation(out=rs.ap[:, 0:N], in_=psum[:, 6, 0:N], func=AF.Sqrt, bias=epsR.ap[:, 0:1], scale=1.0),
            psc(6, 0, N) + epsR.cells(), rs.cells(None))
        dve(lambda e: e.reciprocal(out=rs.ap[:, 0:N], in_=rs.ap[:, 0:N]), rs.cells(), rs.cells())

    def rmsnorm_to_h(N, gname, l, stats_done=False):
        if not stats_done:
            for kc in range(KC):
                sq_stat(N, kc)
        rstd_finish(N)

    def apply_norm_h(N, gname, l):
        for kc in range(KC):
            gcol = pc(gname, l, kc)
            dve(lambda e, kc=kc, gcol=gcol: e.scalar_tensor_tensor(out=h.ap[:, kc, 0:N], in0=xT.ap[:, kc, 0:N], scalar=gcol,
                                                                   in1=rs.ap[:, 0:N], op0=ALU.mult, op1=ALU.mult),
                xT.cells(kc, 0, N) + rs.cells() + pcols.cells(), h.cells(kc, 0, N))

    def proj_fm(slot, N, evac):
        for mi in range(4):
            bank = nb_main()
            mm_group(bank, N,
                     [wbuf[slot].ap[:, kc, mi * 128:(mi + 1) * 128] for kc in range(KC)],
                     [h.ap[:, kc, 0:N] for kc in range(KC)],
                     [wbuf[slot].cells(kc) for kc in range(KC)],
                     [h.cells(kc, 0, N) for kc in range(KC)])
            evac(mi, bank)

    def tile(ti):
        tok0, N = TILES[ti]
        if ti == 0:
            segs = [(0, N, "zero")]
            chunks = [(c * 128, 128, False) for c in range(N // 128)]
        elif ti == NTILE - 1:
            npr = N - NS
            segs = [(0, npr, "carry"), (npr, 16, 0), (npr + 16, 16, 1)]
            chunks = [(c * 128, 128, False) for c in range(npr // 128)] + [(npr, NS, True)]
        else:
            segs = [(0, N, "carry")]
            chunks = [(c * 128, 128, False) for c in range(N // 128)]
        i_pr = max(i for i, sg_ in enumerate(segs) if not isinstance(sg_[2], int))
        last_pc = max(i for i, ch_ in enumerate(chunks) if not ch_[2])

        def eoffs(HL):
            o, out = 0, []
            for (c0, ln, src) in segs:
                out.append(o)
                o += HL + ln
            return out, o
        eA, EWa = eoffs(HA)
        eB, EWb = eoffs(HB)
        eD, EWd = eoffs(HD)
        CWa, CWb = EWa - HA, EWb - HB

        for q in range(4):
            dma_io(xT.ap[:, q * 4:(q + 1) * 4, 0:N], xin[:, q * 4:(q + 1) * 4, tok0:tok0 + N], [],
                   [c for kc in range(q * 4, q * 4 + 4) for c in xT.cells(kc, 0, N)])

        def layer(l):
            base = l * NG
            dma_io(pbc.ap, pbc_d[l], [], pbc.cells())
            dma_io(pbs.ap, pbs_d[l], [], pbs.cells())
            dma_io(pbsS.ap, pbsS_d[l], [], pbsS.cells())
            rmsnorm_to_h(N, "gmix", l, stats_done=(l > 0))
            apply_norm_h(N, "gmix", l)

            for si, (c0, ln, src) in enumerate(segs):
                for (ext, eo, HL, hb, st_d) in ((a_ext, eA, HA, hAb, sta_d), (cgh, eB, HB, hBb, stb_d), (zd, eD, HD, hDb, std_d)):
                    o = eo[si]
                    wcells = [c for ch in range(4) for c in ext.cells(ch, o, o + HL)]
                    if src == "halo":
                        pass
                    elif src == "zero":
                        dve(lambda e, ext=ext, o=o, HL=HL: e.memset(ext.ap[:, :, o:o + HL], 0.0), [], wcells)
                    elif src == "carry":
                        act(lambda e, ext=ext, o=o, HL=HL, hb=hb: e.copy(out=ext.ap[:, :, o:o + HL], in_=hb.ap[:, l, :, :]),
                            hb.cells(), wcells)
                    else:
                        dma_io(ext.ap[:, :, o:o + HL], st_d[l, src], [], wcells)

            chk('%d.%d.p1' % (ti, l))
            def seg_evac(fn_seg, bank, ext, eo, HL, ch, eng, extra_r=()):
                for si, (c0, ln, src) in enumerate(segs):
                    o = eo[si] + HL
                    eng(lambda e, c0=c0, ln=ln, o=o: fn_seg(e, psum[:, bank, c0:c0 + ln], ext.ap[:, ch, o:o + ln], c0, ln),
                        psc(bank, c0, c0 + ln) + list(extra_r), ext.cells(ch, o, o + ln))
                if ti == 0:
                    dve(lambda e: e.tensor_scalar(out=ext.ap[:, ch, HL:HL + HALO], in0=ext.ap[:, ch, HL:HL + HALO],
                                                  scalar1=pcols.ap[:, PCO["flag"]:PCO["flag"] + 1], scalar2=None, op0=ALU.mult),
                        ext.cells(ch, HL, HL + HALO) + pcols.cells(), ext.cells(ch, HL, HL + HALO))

            bg = []

            def drain(n):
                for _ in range(min(n, len(bg))):
                    bg.pop(0)()

            s = load_group(base + GI[("in", 1)])
            proj_fm(s, N, lambda mi, bank: act(
                lambda e: e.activation(out=sgA.ap[:, mi, 0:N], in_=psum[:, bank, 0:N], func=AF.Sigmoid),
                psc(bank, 0, N), sgA.cells(mi, 0, N)))
            s = load_group(base + GI[("in", 0)])

            def evac_a(mi, bank):
                seg_evac(lambda e, p_, o_, c0, ln: e.tensor_tensor(out=o_, in0=p_, in1=sgA.ap[:, mi, c0:c0 + ln], op=ALU.mult),
                         bank, a_ext, eA, HA, mi, dve, sgA.cells(mi, 0, N))
                act(lambda e: e.copy(out=a_bf.ap[:, mi, 0:EWa], in_=a_ext.ap[:, mi, 0:EWa]), a_ext.cells(mi, 0, EWa), a_bf.cells(mi, 0, EWa))
            proj_fm(s, N, evac_a)
            KPE = 16
            PA = PB = PC = False
            engp = pool if (ti >= 1 and PC) else dve
            engb = pool if (ti >= 1 and PB) else dve
            for ch in range(4):
                bank = nb_main()
                for k in range(KPE, 31):
                    wcol = pc("caw", l, ch * 31 + k)
                    slot = dr_rr[0] % 8
                    dr_rr[0] += 1
                    dve(lambda e, slot=slot, wcol=wcol: e.tensor_scalar(out=dring.ap[:, slot, :], in0=identb.ap, scalar1=wcol, scalar2=None, op0=ALU.mult),
                        identb.cells() + pcols.cells(), dring.cells(slot))
                    pe(lambda e, slot=slot, ch=ch, k=k, bank=bank: e.matmul(psum[:, bank, 0:CWa], lhsT=dring.ap[:, slot, :], rhs=a_bf.ap[:, ch, k:k + CWa],
                                                                          start=(k == KPE), stop=(k == 30)),
                       dring.cells(slot) + a_bf.cells(ch, 0, EWa), psc(bank, 0, CWa))
                dve(lambda e, ch=ch, bank=bank: e.tensor_scalar(out=cacc.ap[:, ch, 0:CWa], in0=psum[:, bank, 0:CWa], scalar1=pc("cab", l, ch), scalar2=None, op0=ALU.add),
                    psc(bank, 0, CWa) + pcols.cells() + sgA.cells(), cacc.cells(ch, 0, CWa))
            for ch in range(4):
                for k in range(KPE):
                    wcol = pc("caw", l, ch * 31 + k)
                    bg.append(lambda ch=ch, k=k, wcol=wcol: dve(
                        lambda e: e.scalar_tensor_tensor(
                            out=cacc.ap[:, ch, 0:CWa], in0=a_ext.ap[:, ch, k:k + CWa], scalar=wcol, in1=cacc.ap[:, ch, 0:CWa],
                            op0=ALU.mult, op1=ALU.add),
                        a_ext.cells(ch, 0, EWa) + cacc.cells(ch, 0, CWa) + pcols.cells(), cacc.cells(ch, 0, CWa)))
            drain(22)
            s = load_group(base + GI[("in", 3)])
            proj_fm(s, N, lambda mi, bank: act(
                lambda e: e.copy(out=cgt.ap[:, mi, 0:N], in_=psum[:, bank, 0:N]),
                psc(bank, 0, N), cgt.cells(mi, 0, N)))
            s = load_group(base + GI[("in", 4)])

            def evac_hb(mi, bank):
                seg_evac(lambda e, p_, o_, c0, ln: e.tensor_tensor(out=o_, in0=p_, in1=cgt.ap[:, mi, c0:c0 + ln], op=ALU.mult),
                         bank, cgh, eB, HB, mi, dve, cgt.cells(mi, 0, N))
                drain(5)
            proj_fm(s, N, evac_hb)
            for ch in range(4):
                for k in range(3):
                    wcol = pc("cbw", l, ch * 3 + k)
                    if k == 0:
                        dve(lambda e, ch=ch, wcol=wcol: e.tensor_scalar(out=bconv.ap[:, ch, 0:CWb], in0=cgh.ap[:, ch, 0:CWb],
                                                                        scalar1=wcol, scalar2=None, op0=ALU.mult),
                            cgh.cells(ch, 0, EWb) + pcols.cells(), bconv.cells(ch, 0, CWb))
                    else:
                        dve(lambda e, ch=ch, k=k, wcol=wcol: e.scalar_tensor_tensor(
                            out=bconv.ap[:, ch, 0:CWb], in0=cgh.ap[:, ch, k:k + CWb], scalar=wcol, in1=bconv.ap[:, ch, 0:CWb],
                            op0=ALU.mult, op1=ALU.add),
                            cgh.cells(ch, 0, EWb) + bconv.cells(ch, 0, CWb) + pcols.cells(), bconv.cells(ch, 0, CWb))
            s = load_group(base + GI[("in", 2)])

            def evac_bg(mi, bank):
                for si, (c0, ln, src) in enumerate(segs):
                    o = eB[si]
                    dve(lambda e, c0=c0, ln=ln, o=o: e.tensor_tensor(out=br.ap[:, 1, mi, c0:c0 + ln], in0=psum[:, bank, c0:c0 + ln],
                                                                     in1=bconv.ap[:, mi, o:o + ln], op=ALU.mult),
                        psc(bank, c0, c0 + ln) + bconv.cells(mi, 0, CWb), br.cells(1 * 4 + mi, c0, c0 + ln))
                drain(5)
            proj_fm(s, N, evac_bg)
            drain(22)
            s = load_group(base + GI[("in", 5)])
            proj_fm(s, N, lambda mi, bank: act(
                lambda e: e.activation(out=u.ap[:, mi, 0:N], in_=psum[:, bank, 0:N], func=AF.Gelu),
                psc(bank, 0, N), u.cells(mi, 0, N)))
            s = load_group(base + GI[("in", 6)])
            for ci, (c0, M, is_s) in enumerate(chunks):
                mm_group(4 + ci, 512,
                         [h.ap[:, kc, c0:c0 + M] for kc in range(KC)],
                         [wbuf[s].ap[:, kc, :] for kc in range(KC)],
                         [h.cells(kc, c0, c0 + M) for kc in range(KC)],
                         [wbuf[s].cells(kc) for kc in range(KC)], M=M)
                drain(10)
            drain(len(bg))

            def chain_step(ci, step):
                c0, M, is_s = chunks[ci]
                bank = 4 + ci
                vtb = vt[ci % 2]
                if step == 0:
                    act(lambda e: e.activation(out=vtb.ap[0:M, :], in_=psum[0:M, bank, :], func=AF.Gelu), psc(bank), vtb.cells())
                    for q in range(4):
                        dve(lambda e, q=q: e.bn_stats(out=st6.ap[0:M, q, :], in_=vtb.ap[0:M, q * 128:(q + 1) * 128]), vtb.cells(), st6.cells())
                    dve(lambda e: e.bn_aggr(out=mv.ap[0:M, :], in_=st6.ap[0:M, :, :]), st6.cells(), mv.cells())
                elif step == 1:
                    act(lambda e: e.activation(out=mv.ap[0:M, 1:2], in_=mv.ap[0:M, 1:2], func=AF.Sqrt, bias=epsL.ap[0:M, 0:1], scale=1.0),
                        mv.cells() + epsL.cells(), mv.cells())
                    dve(lambda e: e.reciprocal(out=mv.ap[0:M, 1:2], in_=mv.ap[0:M, 1:2]), mv.cells(), mv.cells())
                    dve(lambda e: e.tensor_scalar(out=vtb.ap[0:M, :], in0=vtb.ap[0:M, :], scalar1=mv.ap[0:M, 0:1], scalar2=mv.ap[0:M, 1:2],
                                                  op0=ALU.subtract, op1=ALU.mult), vtb.cells() + mv.cells(), vtb.cells())
                    dve(lambda e: e.tensor_tensor(out=vtb.ap[0:M, :], in0=vtb.ap[0:M, :], in1=pbc.ap[0:M, 0, :], op=ALU.mult),
                        vtb.cells() + pbc.cells(), vtb.cells())
                    dve(lambda e: e.tensor_tensor(out=vtb.ap[0:M, :], in0=vtb.ap[0:M, :], in1=pbc.ap[0:M, 1, :], op=ALU.add),
                        vtb.cells() + pbc.cells(), vtb.cells())
                elif step == 2:
                    act(lambda e: e.copy(out=vnbf.ap[0:M, ci, :], in_=vtb.ap[0:M, :]), vtb.cells(), vnbf.cells(ci))
                    if is_s:
                        dma_io(ovs[l], vtb.ap[0:NS, :], vtb.cells(), [])
                    elif ti == NTILE - 1 and ci == last_pc:
                        dma_io(ov[l], vtb.ap, vtb.cells(), [])

            def chain_hook(hi, mi):
                if hi < len(chunks) and mi < 3:
                    chain_step(hi, mi)

            def spatial(ci):
                c0, M, is_s = chunks[ci]
                for g in range(4):
                    rhs = WmS.ap[0:M, l, g, :] if is_s else WmT.ap[:, l, g, :]
                    pe(lambda e, g=g, rhs=rhs: e.matmul(psum[:, 7, g * 128:g * 128 + M], lhsT=vnbf.ap[0:M, ci, g * 128:(g + 1) * 128],
                                                        rhs=rhs, start=True, stop=True),
                       vnbf.cells(ci) + (WmS.cells() if is_s else WmT.cells()), psc(7, g * 128, g * 128 + M))
                bsrc = pbsS.ap if is_s else pbs.ap
                p7 = psum[:, 7, :].rearrange("p (g t) -> p g t", g=4)[:, :, 0:M]
                m3 = ex2.ap.rearrange("p (g t) -> p g t", g=4)[:, :, 0:M]
                dve(lambda e: e.tensor_tensor(out=m3, in0=p7, in1=bsrc, op=ALU.add),
                    psc(7) + pbs.cells() + pbsS.cells(), ex2.cells())
                dve(lambda e: e.tensor_tensor(out=br.ap[:, 2, :, c0:c0 + M], in0=m3, in1=u.ap[:, :, c0:c0 + M], op=ALU.mult),
                    ex2.cells() + u.cells(), [c for g in range(4) for c in br.cells(2 * 4 + g, c0, c0 + M)])

            def gate_group(b, mg, hook=None):
                s_ = load_group(base + GI[("gate", b, mg)])
                dst = (lambda mi: (sg.ap[:, b * 4 + mi, 0:N], sg.cells(b * 4 + mi, 0, N))) if b < 3 else \
                      (lambda mi: (sg3.ap[:, mi, 0:N], sg3.cells(mi, 0, N)))

                def ev(mi, bank):
                    o_, c_ = dst(mi)
                    act(lambda e: e.activation(out=o_, in_=psum[:, bank, 0:N], func=AF.Sigmoid), psc(bank, 0, N), c_)
                    if callable(hook):
                        hook(mi)
                    elif hook is not None:
                        chain_hook(hook, mi)
                proj_fm(s_, N, ev)

            def sg_of(b, mi):
                return (sg.ap[:, b * 4 + mi, 0:N], sg.cells(b * 4 + mi, 0, N)) if b < 3 else (sg3.ap[:, mi, 0:N], sg3.cells(mi, 0, N))

            s = load_group(base + GI[("in", 7)])
            def evac_zd(mi, bank):
                seg_evac(lambda e, p_, o_, c0, ln: e.copy(out=o_, in_=p_), bank, zd, eD, HD, mi, act)
                chain_hook(0, mi)
            proj_fm(s, N, evac_zd)
            for (ext, eo, HL, hb, o_s, o_p) in ((a_ext, eA, HA, hAb, oas, oa), (cgh, eB, HB, hBb, obs, ob), (zd, eD, HD, hDb, ods, od)):
                rc = ext.cells()
                for si_, sg_ in enumerate(segs):
                    if isinstance(sg_[2], int):
                        o = eo[si_] + 16
                        dma_io(o_s[l, sg_[2]], ext.ap[:, :, o:o + HL], rc, [])
                o = eo[i_pr] + segs[i_pr][1]
                act(lambda e, ext=ext, o=o, HL=HL, hb=hb: e.copy(out=hb.ap[:, l, :, :], in_=ext.ap[:, :, o:o + HL]), rc, hb.cells())
                if ti == NTILE - 1:
                    dma_io(o_p[l], hb.ap[:, l, :, :], hb.cells(), [])

            for g in range(4):
                src_b, bufs = zd, [a_ext, bconv]
                sh = 1
                for stp in range(g + 1):
                    dst_b = bufs[stp % 2]
                    engb(lambda e, g=g, sh=sh, src_b=src_b, dst_b=dst_b: e.tensor_tensor(
                        out=dst_b.ap[:, g, sh:EWd], in0=src_b.ap[:, g, sh:EWd], in1=src_b.ap[:, g, 0:EWd - sh], op=ALU.add),
                        src_b.cells(g, 0, EWd), dst_b.cells(g, 0, EWd))
                    src_b = dst_b
                    sh *= 2
                w = 2 ** (g + 1)
                if ti == 0:
                    oc_ = HD + HALO
                    dve(lambda e, g=g, src_b=src_b, oc_=oc_: e.tensor_tensor(out=src_b.ap[:, g, oc_:oc_ + 16], in0=src_b.ap[:, g, oc_:oc_ + 16],
                                                                     in1=pcorr.ap[:, g, :], op=ALU.mult),
                        src_b.cells(g, 0, EWd) + pcorr.cells(), src_b.cells(g, 0, EWd))
                for si, (c0, ln, src) in enumerate(segs):
                    o = eD[si] + HD
                    dve(lambda e, g=g, w=w, src_b=src_b, c0=c0, ln=ln, o=o: e.scalar_tensor_tensor(
                        out=pooled.ap[:, g, c0:c0 + ln], in0=src_b.ap[:, g, o:o + ln], scalar=1.0 / w, in1=zd.ap[:, g, o:o + ln],
                        op0=ALU.mult, op1=ALU.subtract),
                        src_b.cells(g, 0, EWd) + zd.cells(g, 0, EWd), pooled.cells(g, c0, c0 + ln))
            chk('%d.%d.p2' % (ti, l))
            nch = len(chunks)
            gate_group(0, 0, hook=1)
            for g in range(4):
                bank = nb_sec()
                pe(lambda e, g=g, bank=bank: e.matmul(psum[:, bank, 0:N], lhsT=pwb.ap[:, l, g, :], rhs=pooled.ap[:, g, 0:N], start=True, stop=True),
                   pwb.cells() + pooled.cells(g, 0, N), psc(bank, 0, N))
                act(lambda e, g=g, bank=bank: e.activation(out=br.ap[:, 3, g, 0:N], in_=psum[:, bank, 0:N], func=AF.Copy, scale=pc("psc", l, g)),
                    psc(bank, 0, N) + pcols.cells(), br.cells(3 * 4 + g, 0, N))
            gate_group(1, 0, hook=2)
            gate_group(2, 0, hook=3)
            for ci in range(nch):
                spatial(ci)

            for ch in range(4):
                act(lambda e, ch=ch: e.copy(out=cbf.ap[:, ch, 0:CWa], in_=cacc.ap[:, ch, 0:CWa]), cacc.cells(ch, 0, CWa), cbf.cells(ch, 0, CWa))
                act(lambda e, ch=ch: e.activation(out=sq.ap[:, ch, 0:CWa], in_=cacc.ap[:, ch, 0:CWa], func=AF.Square),
                    cacc.cells(ch, 0, CWa), sq.cells(ch, 0, CWa))
            for ch in range(4):
                pe(lambda e, ch=ch: e.matmul(psum[:, 6, 0:CWa], lhsT=onesA.ap, rhs=cbf.ap[:, ch, 0:CWa], start=(ch == 0), stop=(ch == 3)),
                   onesA.cells() + cbf.cells(ch, 0, CWa), psc(6, 0, CWa))
            for ch in range(4):
                pe(lambda e, ch=ch: e.matmul(psum[:, 7, 0:CWa], lhsT=onesA.ap, rhs=sq.ap[:, ch, 0:CWa], start=(ch == 0), stop=(ch == 3)),
                   onesA.cells() + sq.cells(ch, 0, CWa), psc(7, 0, CWa))
            def ln_hook(mi):
                if mi == 0:
                    act(lambda e: e.copy(out=mean.ap[:, 0:CWa], in_=psum[:, 6, 0:CWa]), psc(6, 0, CWa), mean.cells())
                    dve(lambda e: e.tensor_tensor(out=ex2.ap[:, 0:CWa], in0=mean.ap[:, 0:CWa], in1=mean.ap[:, 0:CWa], op=ALU.mult), mean.cells(), ex2.cells())
                    dve(lambda e: e.tensor_tensor(out=ex2.ap[:, 0:CWa], in0=psum[:, 7, 0:CWa], in1=ex2.ap[:, 0:CWa], op=ALU.subtract),
                        psc(7, 0, CWa) + ex2.cells(), ex2.cells())
                    dve(lambda e: e.tensor_scalar(out=ex2.ap[:, 0:CWa], in0=ex2.ap[:, 0:CWa], scalar1=0.0, scalar2=None, op0=ALU.max), ex2.cells(), ex2.cells())
                elif mi == 1:
                    act(lambda e: e.activation(out=rstdA.ap[:, 0:CWa], in_=ex2.ap[:, 0:CWa], func=AF.Sqrt, bias=epsL.ap[:, 0:1], scale=1.0),
                        ex2.cells() + epsL.cells(), rstdA.cells())
                    dve(lambda e: e.reciprocal(out=rstdA.ap[:, 0:CWa], in_=rstdA.ap[:, 0:CWa]), rstdA.cells(), rstdA.cells())
                    for ch in range(4):
                        dve(lambda e, ch=ch: e.tensor_tensor(out=cacc.ap[:, ch, 0:CWa], in0=cacc.ap[:, ch, 0:CWa], in1=mean.ap[:, 0:CWa], op=ALU.subtract),
                            cacc.cells(ch, 0, CWa) + mean.cells(), cacc.cells(ch, 0, CWa))
                        dve(lambda e, ch=ch: e.tensor_tensor(out=cacc.ap[:, ch, 0:CWa], in0=cacc.ap[:, ch, 0:CWa], in1=rstdA.ap[:, 0:CWa], op=ALU.mult),
                            cacc.cells(ch, 0, CWa) + rstdA.cells(), cacc.cells(ch, 0, CWa))
                else:
                    for ch in ((0, 1) if mi == 2 else (2, 3)):
                        for si, (c0, ln, src) in enumerate(segs):
                            o = eA[si]
                            act(lambda e, ch=ch, c0=c0, ln=ln, o=o: e.activation(out=br.ap[:, 0, ch, c0:c0 + ln], in_=cacc.ap[:, ch, o:o + ln], func=AF.Silu,
                                                                                 scale=pc("lag", l, ch), bias=pc("lab", l, ch)),
                                cacc.cells(ch, 0, CWa) + pcols.cells(), br.cells(0 * 4 + ch, c0, c0 + ln))
            gate_group(3, 0, hook=ln_hook)
            chk('%d.%d.p3' % (ti, l))
            for mg in range(4):
                if mg > 0:
                    for b in range(4):
                        gate_group(b, mg)
                s = load_group(base + GI[("wo", mg)])
                BORD = (1, 2, 3, 0)
                for bi, b in enumerate(BORD):
                    for mi in range(4):
                        m = mg * 4 + mi
                        bank = 4 + wo_rr[0] % 4
                        wo_rr[0] += 1
                        mm_group(bank, N,
                                 [wbuf[s].ap[:, b * 4 + kc, mi * 128:(mi + 1) * 128] for kc in range(4)],
                                 [br.ap[:, b, kc, 0:N] for kc in range(4)],
                                 [wbuf[s].cells(b * 4 + kc) for kc in range(4)],
                                 [br.cells(b * 4 + kc, 0, N) for kc in range(4)])
                        sga, sgc = sg_of(b, mi)
                        if bi == 0:
                            dve(lambda e, mi=mi, bank=bank, sga=sga: e.tensor_tensor(out=macc.ap[:, mi, 0:N], in0=psum[:, bank, 0:N],
                                                                                    in1=sga, op=ALU.mult),
                                psc(bank, 0, N) + sgc, macc.cells(mi, 0, N))
                        else:
                            tb = (bi * 4 + mi) % 2
                            dve(lambda e, mi=mi, tb=tb, bank=bank, sga=sga: e.tensor_tensor(out=mtmp.ap[:, tb, 0:N], in0=psum[:, bank, 0:N],
                                                                                           in1=sga, op=ALU.mult),
                                psc(bank, 0, N) + sgc, mtmp.cells(tb, 0, N))
                            if bi < 3:
                                dve(lambda e, mi=mi, tb=tb: e.tensor_tensor(out=macc.ap[:, mi, 0:N], in0=macc.ap[:, mi, 0:N], in1=mtmp.ap[:, tb, 0:N], op=ALU.add),
                                    macc.cells(mi, 0, N) + mtmp.cells(tb, 0, N), macc.cells(mi, 0, N))
                            else:
                                dve(lambda e, mi=mi, m=m, tb=tb: e.tensor_tensor(out=mgb.ap[:, m, 0:N], in0=macc.ap[:, mi, 0:N], in1=mtmp.ap[:, tb, 0:N], op=ALU.add),
                                    macc.cells(mi, 0, N) + mtmp.cells(tb, 0, N), mgb.cells(m, 0, N))
            chk('%d.%d.p4' % (ti, l))
            for og in range(4):
                s = load_group(base + GI[("o", og)])
                for mi in range(4):
                    m = og * 4 + mi
                    bank = nb_main()
                    mm_group(bank, N,
                             [wbuf[s].ap[:, kc, mi * 128:(mi + 1) * 128] for kc in range(KC)],
                             [mgb.ap[:, kc, 0:N] for kc in range(KC)],
                             [wbuf[s].cells(kc) for kc in range(KC)],
                             [mgb.cells(kc, 0, N) for kc in range(KC)])
                    dve(lambda e, m=m, bank=bank: e.tensor_tensor(out=xT.ap[:, m, 0:N], in0=xT.ap[:, m, 0:N], in1=psum[:, bank, 0:N], op=ALU.add),
                        xT.cells(m, 0, N) + psc(bank, 0, N), xT.cells(m, 0, N))
                    if m >= 2:
                        sq_stat(N, m - 2)
            sq_stat(N, KC - 2)
            sq_stat(N, KC - 1)
            chk('%d.%d.p5' % (ti, l))
            rmsnorm_to_h(N, "gffn", l, stats_done=True)
            apply_norm_h(N, "gffn", l)
            chk('%d.%d.p6' % (ti, l))
            for fg in range(11):
                s1 = load_group(base + GI[("w1", fg)])
                s3 = load_group(base + GI[("w3", fg)])
                for fi in range(4):
                    f = fg * 4 + fi
                    bA = nb_main()
                    mm_group(bA, N,
                             [wbuf[s1].ap[:, kc, fi * 128:(fi + 1) * 128] for kc in range(KC)],
                             [h.ap[:, kc, 0:N] for kc in range(KC)],
                             [wbuf[s1].cells(kc) for kc in range(KC)], [h.cells(kc, 0, N) for kc in range(KC)])
                    act(lambda e, f=f, bA=bA: e.activation(out=ftmp.ap[:, f % 2, 0:N], in_=psum[:, bA, 0:N], func=AF.Silu),
                        psc(bA, 0, N), ftmp.cells(f % 2, 0, N))
                    bB = nb_sec()
                    mm_group(bB, N,
                             [wbuf[s3].ap[:, kc, fi * 128:(fi + 1) * 128] for kc in range(KC)],
                             [h.ap[:, kc, 0:N] for kc in range(KC)],
                             [wbuf[s3].cells(kc) for kc in range(KC)], [h.cells(kc, 0, N) for kc in range(KC)])
                    dve(lambda e, f=f, bB=bB: e.tensor_tensor(out=hid.ap[:, f, 0:N], in0=psum[:, bB, 0:N], in1=ftmp.ap[:, f % 2, 0:N], op=ALU.mult),
                        psc(bB, 0, N) + ftmp.cells(f % 2, 0, N), hid.cells(f, 0, N))
            for og in range(4):
                for kg in range(4):
                    if kg == 1 and og > 0:
                        for m in range((og - 1) * 4, og * 4):
                            sq_stat(N, m)
                    s = load_group(base + GI[("w2", og, kg)])
                    for mi in range(4):
                        for j in range(11):
                            f = kg * 11 + j
                            pe(lambda e, mi=mi, j=j, f=f, s=s, kg=kg: e.matmul(psum[:, mi, 0:N], lhsT=wbuf[s].ap[:, j, mi * 128:(mi + 1) * 128],
                                                                              rhs=hid.ap[:, f, 0:N], start=(kg == 0 and j == 0), stop=(kg == 3 and j == 10)),
                               wbuf[s].cells(j) + hid.cells(f, 0, N), psc(mi, 0, N))
                for mi in range(4):
                    m = og * 4 + mi
                    dve(lambda e, m=m, mi=mi: e.tensor_tensor(out=xT.ap[:, m, 0:N], in0=xT.ap[:, m, 0:N], in1=psum[:, mi, 0:N], op=ALU.add),
                        xT.cells(m, 0, N) + psc(mi, 0, N), xT.cells(m, 0, N))
                bank_main[0] = 0
            for m in range(12, KC):
                sq_stat(N, m)
        for l_ in range(L):
            layer(l_)
        chk('%d.p7' % ti)
        rmsnorm_to_h(N, "gfin", None, stats_done=True)
        for q in range(4):
            for j in range(4):
                kc = q * 4 + j
                gcol = pc("gfin", None, kc)
                dve(lambda e, kc=kc, j=j, q=q, gcol=gcol: e.scalar_tensor_tensor(out=ystg.ap[:, q % 2, j, 0:N], in0=xT.ap[:, kc, 0:N], scalar=gcol,
                                                                                 in1=rs.ap[:, 0:N], op0=ALU.mult, op1=ALU.mult),
                    xT.cells(kc, 0, N) + rs.cells() + pcols.cells(), sbc(ystg.off + (q % 2) * 4 * NT * 4 + j * NT * 4, N * 4))
            dma_io(yT[:, q * 4:(q + 1) * 4, tok0:tok0 + N], ystg.ap[:, q % 2, :, 0:N], sbc(ystg.off + (q % 2) * 4 * NT * 4, 4 * NT * 4), [])

    for ti in range(NTILE):
        tile(ti)
        chk('%d.end' % ti)

    cnt = {e: 0 for e in sem_eng}
    dcum = {}
    dlast = {}
    for I in P.ins:
        if I.dma:
            k = id(I.dsem)
            I.dprev = dcum.get(k, 0)
            I.dval = I.dprev + 16
            dcum[k] = I.dval
        elif I.signal:
            cnt[I.eng] += 1
            I.count = cnt[I.eng]

    streams = {e: [] for e in ("pe", "act", "dve", "pool", "sp")}
    for I in P.ins:
        streams[I.eng].append(I)
    final_io = [(s, dcum.get(id(s), 0)) for s in sem_io]

    def emit(engname, e):
        waited = {}

        def wait(sem, val):
            k = id(sem)
            if waited.get(k, 0) >= val:
                return
            waited[k] = val
            e.wait_ge(sem, val)
        for I in streams[engname]:
            need = {}
            for d in I.deps:
                sm, v = (d.dsem, d.dval) if d.dma else (sem_eng[d.eng], d.count)
                if need.get(id(sm), (None, 0))[1] < v:
                    need[id(sm)] = (sm, v)
            for (sm, v) in need.values():
                wait(sm, v)
            if I.dma:
                if I.dprev:
                    wait(I.dsem, I.dprev)
                I.fn(e).then_inc(I.dsem, 16)
            else:
                bi = I.fn(e)
                if I.signal:
                    bi.then_inc(sem_eng[I.eng], 1)
        if engname == "act":
            for (s, v) in final_io:
                if v:
                    wait(s, v)

    with nc.Block() as block:
        @block.tensor
        def _(e):
            emit("pe", e)

        @block.scalar
        def _(e):
            emit("act", e)

        @block.vector
        def _(e):
            emit("dve", e)

        @block.gpsimd
        def _(e):
            emit("pool", e)

        @block.sync
        def _(e):
            emit("sp", e)
    es.close()
    return nc


def _rep(a):
    return np.ascontiguousarray(np.broadcast_to(a[None], (128,) + a.shape))


def kernel(x_prompt, x_sample, state_conv_a, state_conv_b, state_pool, norm_mix_g, w_in, conv_a_w,
           conv_a_b, ln_a_g, ln_a_b, w_out_a, conv_b_w, w_out_b, ln_c_g, ln_c_b, spatial_w, spatial_b,
           w_out_c, pool_w, pool_scale, w_out_d, w_o, norm_ffn_g, ffn_w1, ffn_w3, ffn_w2, norm_final_g):
    f = lambda a: np.ascontiguousarray(np.asarray(a, dtype=np.float32))
    x_prompt, x_sample = f(x_prompt), f(x_sample)
    col = lambda v, n: np.asarray(v, np.float32).reshape(n, 128).T
    shared = {
        "w_in": f(w_in), "w_out_a": f(w_out_a), "w_out_b": f(w_out_b), "w_out_c": f(w_out_c), "w_out_d": f(w_out_d),
        "w_o": f(w_o), "ffn_w1": f(ffn_w1), "ffn_w3": f(ffn_w3), "ffn_w2": f(ffn_w2),
    }
    pcb = np.zeros((128, NPC), np.float32)
    for l in range(L):
        pcb[:, PCO[("gmix", l)]:PCO[("gmix", l)] + 16] = col(norm_mix_g[l], 16)
        pcb[:, PCO[("gffn", l)]:PCO[("gffn", l)] + 16] = col(norm_ffn_g[l], 16)
        caw = np.asarray(conv_a_w[l], np.float32)
        pcb[:, PCO[("caw", l)]:PCO[("caw", l)] + 124] = caw.reshape(31, 4, 128).transpose(2, 1, 0).reshape(128, 124)
        pcb[:, PCO[("cab", l)]:PCO[("cab", l)] + 4] = col(conv_a_b[l], 4)
        pcb[:, PCO[("lag", l)]:PCO[("lag", l)] + 4] = col(ln_a_g[l], 4)
        pcb[:, PCO[("lab", l)]:PCO[("lab", l)] + 4] = col(ln_a_b[l], 4)
        cbw = np.asarray(conv_b_w[l], np.float32)
        pcb[:, PCO[("cbw", l)]:PCO[("cbw", l)] + 12] = cbw.reshape(3, 4, 128).transpose(2, 1, 0).reshape(128, 12)
        pcb[:, PCO[("psc", l)]:PCO[("psc", l)] + 4] = col(pool_scale[l], 4)
    pcb[:, PCO["gfin"]:PCO["gfin"] + 16] = col(norm_final_g, 16)
    pbc = np.stack([np.stack([_rep(f(ln_c_g)[l]), _rep(f(ln_c_b)[l])], axis=1) for l in range(L)])
    sb = f(spatial_b)
    pbs = np.stack([_rep(sb[l]) for l in range(L)])
    pbsS = np.stack([_rep(np.concatenate([sb[l][:, :16], sb[l][:, :16]], axis=1)) for l in range(L)])
    sw = f(spatial_w)
    wsT = np.ascontiguousarray(sw.transpose(3, 0, 1, 2))
    wsS = np.zeros((32, L, 4, 32), np.float32)
    for q in range(2):
        wsS[q * 16:(q + 1) * 16, :, :, q * 16:(q + 1) * 16] = wsT[:16, :, :, :16]
    maskP = np.triu(np.ones((128, 128), np.float32))
    maskS = np.zeros((32, 32), np.float32)
    for q in range(2):
        maskS[q * 16:(q + 1) * 16, q * 16:(q + 1) * 16] = np.triu(np.ones((16, 16), np.float32))
    pw = np.ascontiguousarray(f(pool_w).transpose(2, 0, 1, 3))

    def st_layout(s, HL):
        s = f(s).reshape(L, 16, HL, 4, 128).transpose(0, 1, 4, 3, 2)
        return np.ascontiguousarray(s)
    sta_all, stb_all, std_all = st_layout(state_conv_a, HA), st_layout(state_conv_b, HB), st_layout(state_pool, HD)

    in_maps = []
    for c in range(NCORE):
        b, sgi = c // 4, c % 4
        toks = np.zeros((NTOK, D), np.float32)
        if sgi > 0:
            toks[0:HALO] = x_prompt[b, sgi * SEG - HALO:sgi * SEG]
        toks[PEND:PEND + 16] = x_sample[2 * c]
        toks[PEND + 16:PEND + 32] = x_sample[2 * c + 1]
        toks[HALO:PEND] = x_prompt[b, sgi * SEG:(sgi + 1) * SEG]
        xin = np.ascontiguousarray(toks.reshape(NTOK, KC, 128).transpose(2, 1, 0))
        pc_c = pcb.copy()
        pc_c[:, PCO["flag"]] = 0.0 if sgi == 0 else 1.0
        pcorr = np.ones((128, 4, 16), np.float32)
        if sgi == 0:
            for g, w in enumerate((2, 4, 8, 16)):
                for t in range(16):
                    pcorr[:, g, t] = np.float32(w) / np.float32(min(t + 1, w))
        m = dict(shared)
        m.update({
            "xin": xin, "pcols": pc_c, "pbc": pbc, "pbs": pbs, "pbsS": pbsS, "wsT": wsT, "wsS": wsS,
            "maskP": maskP, "maskS": maskS, "pw": pw, "pcorr": pcorr, "ident": np.eye(128, dtype=np.float32),
            "sta": np.ascontiguousarray(sta_all[:, 2 * c:2 * c + 2]),
            "stb": np.ascontiguousarray(stb_all[:, 2 * c:2 * c + 2]),
            "std": np.ascontiguousarray(std_all[:, 2 * c:2 * c + 2]),
        })
        in_maps.append(m)

    nc = build_nc()
    res = run_bass_kernel_spmd(nc, in_maps, core_ids=list(range(NCORE)))
    R = list(res.results)

    y_prompt = np.zeros((2, 4 * SEG, D), np.float32)
    y_sample = np.zeros((16, 16, D), np.float32)
    for c in range(NCORE):
        b, sgi = c // 4, c % 4
        yt = np.asarray(R[c]["yT"]).transpose(2, 1, 0).reshape(NTOK, D)
        y_prompt[b, sgi * SEG:(sgi + 1) * SEG] = yt[HALO:PEND]
        y_sample[2 * c] = yt[PEND:PEND + 16]
        y_sample[2 * c + 1] = yt[PEND + 16:PEND + 32]

    def st_back(a):
        return np.asarray(a).transpose(2, 1, 0).reshape(a.shape[2], 512)
    outs = []
    for (name_p, name_s, HL) in (("oa", "oas", HA), ("ob", "obs", HB), ("od", "ods", HD)):
        p = np.zeros((L, 2, HL, 512), np.float32)
        s = np.zeros((L, 16, HL, 512), np.float32)
        for l in range(L):
            for b in range(2):
                p[l, b] = st_back(R[b * 4 + 3][name_p][l])
            for c in range(NCORE):
                for q in range(2):
                    s[l, 2 * c + q] = st_back(R[c][name_s][l, q])
        outs += [p, s]
    vp = np.zeros((L, 2, 128, 512), np.float32)
    vs = np.zeros((L, 16, 16, 512), np.float32)
    for l in range(L):
        for b in range(2):
            vp[l, b] = np.asarray(R[b * 4 + 3]["ov"][l])
        for c in range(NCORE):
            o = np.asarray(R[c]["ovs"][l])
            vs[l, 2 * c] = o[0:16]
            vs[l, 2 * c + 1] = o[16:32]
    return (y_prompt, y_sample, outs[0], outs[1], outs[2], outs[3], outs[4], outs[5], vp, vs)
```

```python
import numpy as np
import concourse.bass as bass
import concourse.mybir as mybir
from concourse.bass_utils import run_bass_kernel_spmd

F32 = mybir.dt.float32
BF16 = mybir.dt.bfloat16
U8 = mybir.dt.uint8
AF = mybir.ActivationFunctionType
ALU = mybir.AluOpType

L = 2
D = 2048
KC = 16
DFF = 5632
FC = 44
NCORE = 8
SEG = 4096
HALO = 128
NS = 32
T0 = HALO + NS
NT = 512
NTOK_ = T0 + SEG
PEND = HALO + SEG
TILES = [(i * NT, NT) for i in range(8)] + [(8 * NT, NTOK_ - 8 * NT)]
NTILE = len(TILES)
NTOK = T0 + SEG
HA, HB, HD = 30, 2, 15
RMS_EPS = 1e-6
LN_EPS = 1e-5
NG = 70
NSLOT = 3
SBCELL = 256
PSCELL = 512


class Ins:
    __slots__ = ("eng", "fn", "reads", "writes", "dma", "deps", "signal", "count", "dsem", "dval", "dprev", "idx")

    def __init__(self, eng, fn, reads, writes, dma):
        self.eng, self.fn, self.reads, self.writes, self.dma = eng, fn, reads, writes, dma
        self.deps = []
        self.signal = False
        self.count = 0
        self.dsem = None
        self.dval = 0
        self.dprev = 0


class Prog:
    def __init__(self):
        self.ins = []
        self.lastw = {}
        self.readers = {}
        self.stopped = False

    def add(self, eng, fn, reads=(), writes=(), dma=False, dsem=None):
        I = Ins(eng, fn, reads, writes, dma)
        if self.stopped:
            return I
        I.dsem = dsem
        I.idx = len(self.ins)
        cdeps = {}
        ddeps = {}
        lastw, readers = self.lastw, self.readers

        def dep(w):
            if w is I:
                return
            if w.dma:
                ddeps[w.idx] = w
            else:
                c = cdeps.get(w.eng)
                if c is None or c.idx < w.idx:
                    cdeps[w.eng] = w
        for k in reads:
            w = lastw.get(k)
            if w is not None:
                dep(w)
        for k in writes:
            w = lastw.get(k)
            if w is not None and (w.dma or dma or w.eng != eng or eng != "pe"):
                dep(w)
            rs = readers.get(k)
            if rs:
                for r in rs.values():
                    if r.dma or dma or r.eng != eng or eng != "pe":
                        dep(r)
        for k in reads:
            rd = readers.get(k)
            if rd is None:
                rd = readers[k] = {}
            rd[("dma", I.idx) if dma else eng] = I
        for k in writes:
            lastw[k] = I
            readers[k] = {}
        I.deps = list(cdeps.values()) + list(ddeps.values())
        for d in cdeps.values():
            d.signal = True
        self.ins.append(I)
        return I


def sbc(off, nbytes):
    return [("s", c) for c in range(off // SBCELL, (off + nbytes - 1) // SBCELL + 1)]


class Buf:
    def __init__(self, arena, off, shape, dt):
        self.off = off
        self.shape = shape
        self.dt = dt
        self.esz = 4 if dt == F32 else 2
        n = int(np.prod(shape))
        self.nbytes = n * self.esz
        ap = arena[:, off:off + self.nbytes].bitcast(dt)
        if len(shape) == 2:
            ap = ap.rearrange("p (a b) -> p a b", a=shape[0])
        elif len(shape) == 3:
            ap = ap.rearrange("p (a b c) -> p a b c", a=shape[0], b=shape[1])
        self.ap = ap
        self.row = shape[-1] * self.esz if len(shape) > 1 else self.nbytes

    def cells(self, i=None, lo=None, hi=None):
        if i is None:
            return sbc(self.off, self.nbytes)
        base = self.off + i * self.row
        if lo is None:
            return sbc(base, self.row)
        return sbc(base + lo * self.esz, (hi - lo) * self.esz)


def psc(bank, lo=0, hi=512):
    b0 = bank * 2048 + lo * 4
    b1 = bank * 2048 + hi * 4 - 1
    return [("p", c) for c in range(b0 // PSCELL, b1 // PSCELL + 1)]


def pcol_layout():
    off = {}
    n = 0
    for l in range(L):
        for name, w in (("gmix", 16), ("gffn", 16), ("caw", 4 * 31), ("cab", 4), ("lag", 4), ("lab", 4),
                        ("cbw", 4 * 3), ("psc", 4)):
            off[(name, l)] = n
            n += w
    off["gfin"] = n
    n += 16
    off["flag"] = n
    n += 1
    return off, n


PCO, NPC = pcol_layout()


def group_index():
    gi = {}
    n = 0
    for j in range(8):
        gi[("in", j)] = n; n += 1
    for mg in range(4):
        for b in range(4):
            gi[("gate", b, mg)] = n; n += 1
        gi[("wo", mg)] = n; n += 1
    for og in range(4):
        gi[("o", og)] = n; n += 1
    for fg in range(11):
        gi[("w1", fg)] = n; n += 1
        gi[("w3", fg)] = n; n += 1
    for og in range(4):
        for kg in range(4):
            gi[("w2", og, kg)] = n; n += 1
    assert n == NG
    return gi


GI = group_index()


def build_nc():
    nc = bass.Bass("TRN2", target_bir_lowering=False)
    dt_in = lambda name, shape: nc.dram_tensor(name, shape, F32, kind="ExternalInput").ap()
    dt_out = lambda name, shape: nc.dram_tensor(name, shape, F32, kind="ExternalOutput").ap()
    xin = dt_in("xin", [128, KC, NTOK])
    w_in = dt_in("w_in", [L, D, 12288])
    w_oa = [dt_in("w_out_" + c, [L, 512, D]) for c in "abcd"]
    w_o = dt_in("w_o", [L, D, D])
    w1 = dt_in("ffn_w1", [L, D, DFF])
    w3 = dt_in("ffn_w3", [L, D, DFF])
    w2 = dt_in("ffn_w2", [L, DFF, D])
    pcols_d = dt_in("pcols", [128, NPC])
    pbc_d = dt_in("pbc", [L, 128, 2, 512])
    pbs_d = dt_in("pbs", [L, 128, 4, 128])
    pbsS_d = dt_in("pbsS", [L, 128, 4, 32])
    wsT_d = dt_in("wsT", [128, L, 4, 128])
    wsS_d = dt_in("wsS", [32, L, 4, 32])
    maskP_d = dt_in("maskP", [128, 128])
    maskS_d = dt_in("maskS", [32, 32])
    pw_d = dt_in("pw", [128, L, 4, 128])
    pcorr_d = dt_in("pcorr", [128, 4, 16])
    sta_d = dt_in("sta", [L, 2, 128, 4, HA])
    stb_d = dt_in("stb", [L, 2, 128, 4, HB])
    std_d = dt_in("std", [L, 2, 128, 4, HD])
    yT = dt_out("yT", [128, KC, NTOK])
    oa = dt_out("oa", [L, 128, 4, HA])
    ob = dt_out("ob", [L, 128, 4, HB])
    od = dt_out("od", [L, 128, 4, HD])
    oas = dt_out("oas", [L, 2, 128, 4, HA])
    obs = dt_out("obs", [L, 2, 128, 4, HB])
    ods = dt_out("ods", [L, 2, 128, 4, HD])
    ov = dt_out("ov", [L, 128, 512])
    ovs = dt_out("ovs", [L, NS, 512])
    wsc_l = [nc.dram_tensor("wsc%d" % l, [NG, 128, 16, 512], BF16, kind="Internal").ap() for l in range(L)]

    class _W:
        def __getitem__(self, g):
            return wsc_l[g // NG][g % NG]
    wsc = _W()

    P = Prog()
    STOP = ""

    def chk(tag):
        if STOP and STOP == tag:
            P.stopped = True
    from contextlib import ExitStack
    es = ExitStack()
    ARENA_BYTES = 210944
    arena = es.enter_context(nc.sbuf_tensor("arena", [128, ARENA_BYTES], U8))
    psum = es.enter_context(nc.psum_tensor("psum", [128, 8, 512], F32))
    sem_eng = {e: es.enter_context(nc.semaphore("sem_" + e)) for e in ("pe", "act", "dve", "pool")}
    sem_slot = [es.enter_context(nc.semaphore("sem_slot%d" % i)) for i in range(NSLOT)]
    sem_cv = [es.enter_context(nc.semaphore("sem_cv%d" % i)) for i in range(6)]
    sem_io = [es.enter_context(nc.semaphore("sem_io%d" % i)) for i in range(8)]
    sem_st = [es.enter_context(nc.semaphore("sem_st%d" % i)) for i in range(4)]

    cur = [0]

    def alloc(shape, dt, at=None):
        esz = 4 if dt == F32 else 2
        nb = int(np.prod(shape)) * esz
        if at is None:
            off = cur[0]
            cur[0] = (off + nb + SBCELL - 1) // SBCELL * SBCELL
        else:
            off = at
        return Buf(arena, off, shape, dt)

    xT = alloc([KC, NT], F32)
    h = alloc([KC, NT], BF16)
    wbuf = [alloc([16, 512], BF16) for _ in range(NSLOT)]
    pcols = alloc([NPC], F32)
    pbc = alloc([2, 512], F32)
    pbs = alloc([4, 128], F32)
    pbsS = alloc([4, 32], F32)
    WmT = alloc([L, 4, 128], BF16)
    WmS = alloc([L, 4, 32], BF16)
    pwb = alloc([L, 4, 128], BF16)
    onesD = alloc([128], BF16)
    onesA = alloc([128], BF16)
    epsR = alloc([1], F32)
    epsL = alloc([1], F32)
    pcorr = alloc([4, 16], F32)
    hAb = alloc([L, 4, HA], F32)
    hBb = alloc([L, 4, HB], F32)
    hDb = alloc([L, 4, HD], F32)
    st6 = alloc([4, 6], F32)
    mv = alloc([2], F32)
    R0 = cur[0]
    br = alloc([4, 4, NT], BF16)
    EWA, EWB, EWD = HA + NT, HB + NT, HD + NT
    a_ext = alloc([4, EWA], F32)
    cgh = alloc([4, EWB], F32)
    zd = alloc([4, EWD], F32)
    u = alloc([4, NT], BF16)
    cacc = alloc([4, 528], F32)
    bconv = alloc([4, 528], F32)
    cbf = alloc([4, NT], BF16)
    sq = alloc([4, NT], BF16)
    mean = alloc([NT], F32)
    ex2 = alloc([NT], F32)
    rstdA = alloc([NT], F32)
    vnbf = alloc([4, 512], BF16)
    sg = alloc([12, NT], BF16)
    ptmp = alloc([528], F32)
    REND = cur[0]
    assert REND <= ARENA_BYTES, REND
    sgA = alloc([4, NT], F32, at=cacc.off)
    cgt = alloc([4, NT], F32, at=cbf.off)
    pooled = alloc([4, NT], BF16, at=cbf.off)
    vt = [alloc([512], F32, at=mean.off), alloc([512], F32, at=rstdA.off)]
    sg3 = alloc([4, NT], BF16, at=u.off)
    mgb = alloc([KC, NT], BF16, at=a_ext.off)
    assert mgb.off + mgb.nbytes <= u.off
    macc = alloc([4, NT], F32, at=bconv.off)
    mtmp = alloc([2, NT], F32, at=cbf.off)
    hid = alloc([FC, NT], BF16, at=R0)
    assert hid.off + hid.nbytes <= cacc.off
    ftmp = alloc([2, NT], F32, at=cacc.off)
    sqb = alloc([4, NT], BF16, at=cacc.off + 4096)
    ystg = alloc([2, 4, NT], F32, at=cacc.off + 8192)
    assert ystg.off + ystg.nbytes <= mean.off
    rs = alloc([NT], F32, at=ex2.off)
    wsTf = alloc([L, 4, 128], F32, at=R0)
    wsSf = alloc([L, 4, 32], F32, at=R0 + 4096)
    mP = alloc([128], F32, at=R0 + 8192)
    mS = alloc([32], F32, at=R0 + 8192 + 512)

    ps = lambda b: psum[:, b, :]

    io_rr = [0]
    cv_rr = [0]
    gres = {}

    def pe(fn, r, w):
        return P.add("pe", fn, r, w)

    def act(fn, r, w):
        return P.add("act", fn, r, w)

    def dve(fn, r, w):
        return P.add("dve", fn, r, w)

    def pool(fn, r, w):
        return P.add("pool", fn, r, w)

    def dma_io(out, in_, r, w):
        s = sem_io[io_rr[0] % len(sem_io)]
        io_rr[0] += 1
        return P.add("act", lambda e, o=out, i=in_: e.dma_start(out=o, in_=i), r, w, dma=True, dsem=s)

    def dma_cv(out, in_, r, w, split=True):
        nk = out.shape[1] if split else 0
        pieces = [(out, in_)] if (not split or nk <= 4) else [(out[:, a:min(a + 4, nk)], in_[:, a:min(a + 4, nk)]) for a in range(0, nk, 4)]
        last = None
        for (o_, i_) in pieces:
            s = sem_cv[cv_rr[0] % len(sem_cv)]
            cv_rr[0] += 1
            w_ = list(w)
            if w and w[0][0] == "d":
                res = ("d", w[0][1], cv_rr[0])
                gres.setdefault(w[0][1], []).append(res)
                w_ = [res]
            last = P.add("pool", lambda e, o=o_, i=i_: e.dma_start(out=o, in_=i), r, w_, dma=True, dsem=s)
        return last

    slot_rr = [0]

    converted = set()
    st_rr = [0]

    def load_group(g):
        s = slot_rr[0] % NSLOT
        slot_rr[0] += 1
        if g not in converted:
            converted.add(g)
            for (dlo, srcv) in gsrc[g]:
                nk = srcv.shape[1]
                for a_ in range(0, nk, 4):
                    b_ = min(a_ + 4, nk)
                    sm = sem_cv[cv_rr[0] % len(sem_cv)]
                    cv_rr[0] += 1
                    P.add("pool", lambda e, o=wbuf[s].ap[:, dlo + a_:dlo + b_, :], i=srcv[:, a_:b_]: e.dma_start(out=o, in_=i),
                          [], [c for kc in range(dlo + a_, dlo + b_) for c in wbuf[s].cells(kc)], dma=True, dsem=sm)
            sm = sem_st[st_rr[0] % len(sem_st)]
            st_rr[0] += 1
            P.add("sp", lambda e, o=wsc[g], i=wbuf[s].ap: e.dma_start(out=o, in_=i),
                  wbuf[s].cells(), [("d", g)], dma=True, dsem=sm)
        else:
            P.add("sp", lambda e, o=wbuf[s].ap, i=wsc[g]: e.dma_start(out=o, in_=i),
                  [("d", g)], wbuf[s].cells(), dma=True, dsem=sem_slot[s])
        return s

    def pc(name, l=None, j=0):
        o = PCO[(name, l)] if l is not None else PCO[name]
        return pcols.ap[:, o + j:o + j + 1]

    dma_io(pcols.ap, pcols_d, [], pcols.cells())
    dma_io(pcorr.ap, pcorr_d, [], pcorr.cells())
    dma_io(wsTf.ap, wsT_d, [], wsTf.cells())
    dma_io(wsSf.ap[0:32], wsS_d, [], wsSf.cells())
    dma_io(mP.ap, maskP_d, [], mP.cells())
    dma_io(mS.ap[0:32], maskS_d, [], mS.cells())
    dma_cv(pwb.ap, pw_d, [], pwb.cells(), split=False)
    chk('init')
    dve(lambda e: e.memset(onesD.ap, 1.0 / D), [], onesD.cells())
    dve(lambda e: e.memset(onesA.ap, 1.0 / 512), [], onesA.cells())
    dve(lambda e: e.memset(epsR.ap, RMS_EPS), [], epsR.cells())
    dve(lambda e: e.memset(epsL.ap, LN_EPS), [], epsL.cells())
    for l in range(L):
        for g in range(4):
            dve(lambda e, l=l, g=g: e.tensor_tensor(out=WmT.ap[:, l, g, :], in0=wsTf.ap[:, l, g, :], in1=mP.ap, op=ALU.mult),
                wsTf.cells() + mP.cells(), WmT.cells())
            dve(lambda e, l=l, g=g: e.tensor_tensor(out=WmS.ap[0:32, l, g, :], in0=wsSf.ap[0:32, l, g, :], in1=mS.ap[0:32], op=ALU.mult),
                wsSf.cells() + mS.cells(), WmS.cells())

    def kview(ap, kc):
        return ap.rearrange("(kc p) n -> p kc n", p=128)

    gsrc = {}
    for l in range(L):
        base = l * NG
        for j in range(8):
            g = base + GI[("in", j)]
            gsrc.setdefault(g, []).append((0, kview(w_in[l][:, j * 512:(j + 1) * 512], 16)))
        for mg in range(4):
            for b in range(4):
                g = base + GI[("gate", b, mg)]
                c0 = 4096 + b * 2048 + mg * 512
                gsrc.setdefault(g, []).append((0, kview(w_in[l][:, c0:c0 + 512], 16)))
            g = base + GI[("wo", mg)]
            for b in range(4):
                gsrc.setdefault(g, []).append((b * 4, kview(w_oa[b][l][:, mg * 512:(mg + 1) * 512], 4)))
        for og in range(4):
            g = base + GI[("o", og)]
            gsrc.setdefault(g, []).append((0, kview(w_o[l][:, og * 512:(og + 1) * 512], 16)))
        for fg in range(11):
            g = base + GI[("w1", fg)]
            gsrc.setdefault(g, []).append((0, kview(w1[l][:, fg * 512:(fg + 1) * 512], 16)))
            g = base + GI[("w3", fg)]
            gsrc.setdefault(g, []).append((0, kview(w3[l][:, fg * 512:(fg + 1) * 512], 16)))
        for og in range(4):
            for kg in range(4):
                g = base + GI[("w2", og, kg)]
                gsrc.setdefault(g, []).append((0, kview(w2[l][kg * 1408:(kg + 1) * 1408, og * 512:(og + 1) * 512], 11)))

    chk('pre')
    bank_main = [0]
    bank_sec = [0]
    wo_rr = [0]

    def nb_main():
        b = bank_main[0] % 4
        bank_main[0] += 1
        return b

    def nb_sec():
        b = 4 + bank_sec[0] % 2
        bank_sec[0] += 1
        return b

    def mm_group(bank, N, lhs_list, rhs_list, lhs_cells, rhs_cells, M=128):
        n = len(lhs_list)
        for k in range(n):
            pe(lambda e, k=k, o=psum[0:M, bank, 0:N], a=lhs_list[k], b=rhs_list[k], st=(k == 0), sp_=(k == n - 1):
               e.matmul(o, lhsT=a, rhs=b, start=st, stop=sp_),
               lhs_cells[k] + rhs_cells[k], psc(bank, 0, N))

    def sq_stat(N, kc):
        act(lambda e: e.activation(out=sqb.ap[:, kc % 4, 0:N], in_=xT.ap[:, kc, 0:N], func=AF.Square),
            xT.cells(kc, 0, N), sqb.cells(kc % 4, 0, N))
        pe(lambda e: e.matmul(psum[:, 6, 0:N], lhsT=onesD.ap, rhs=sqb.ap[:, kc % 4, 0:N], start=(kc == 0), stop=(kc == KC - 1)),
           onesD.cells() + sqb.cells(kc % 4, 0, N), psc(6, 0, N))

    def rstd_finish(N):
        act(lambda e: e.activation(out=rs.ap[:, 0:N], in_=psum[:, 6, 0:N], func=AF.Sqrt, bias=epsR.ap[:, 0:1], scale=1.0),
            psc(6, 0, N) + epsR.cells(), rs.cells(None))
        dve(lambda e: e.reciprocal(out=rs.ap[:, 0:N], in_=rs.ap[:, 0:N]), rs.cells(), rs.cells())

    def rmsnorm_to_h(N, gname, l, stats_done=False):
        if not stats_done:
            for kc in range(KC):
                sq_stat(N, kc)
        rstd_finish(N)

    def apply_norm_h(N, gname, l):
        for kc in range(KC):
            gcol = pc(gname, l, kc)
            dve(lambda e, kc=kc, gcol=gcol: e.scalar_tensor_tensor(out=h.ap[:, kc, 0:N], in0=xT.ap[:, kc, 0:N], scalar=gcol,
                                                                   in1=rs.ap[:, 0:N], op0=ALU.mult, op1=ALU.mult),
                xT.cells(kc, 0, N) + rs.cells() + pcols.cells(), h.cells(kc, 0, N))

    def proj_fm(slot, N, evac):
        for mi in range(4):
            bank = nb_main()
            mm_group(bank, N,
                     [wbuf[slot].ap[:, kc, mi * 128:(mi + 1) * 128] for kc in range(KC)],
                     [h.ap[:, kc, 0:N] for kc in range(KC)],
                     [wbuf[slot].cells(kc) for kc in range(KC)],
                     [h.cells(kc, 0, N) for kc in range(KC)])
            evac(mi, bank)

    def tile(ti):
        tok0, N = TILES[ti]
        if ti == 0:
            segs = [(0, N, "zero")]
            chunks = [(c * 128, 128, False) for c in range(N // 128)]
        elif ti == NTILE - 1:
            npr = N - NS
            segs = [(0, npr, "carry"), (npr, 16, 0), (npr + 16, 16, 1)]
            chunks = [(c * 128, 128, False) for c in range(npr // 128)] + [(npr, NS, True)]
        else:
            segs = [(0, N, "carry")]
            chunks = [(c * 128, 128, False) for c in range(N // 128)]
        i_pr = max(i for i, sg_ in enumerate(segs) if not isinstance(sg_[2], int))
        last_pc = max(i for i, ch_ in enumerate(chunks) if not ch_[2])

        def eoffs(HL):
            o, out = 0, []
            for (c0, ln, src) in segs:
                out.append(o)
                o += HL + ln
            return out, o
        eA, EWa = eoffs(HA)
        eB, EWb = eoffs(HB)
        eD, EWd = eoffs(HD)
        CWa, CWb = EWa - HA, EWb - HB

        for q in range(4):
            dma_io(xT.ap[:, q * 4:(q + 1) * 4, 0:N], xin[:, q * 4:(q + 1) * 4, tok0:tok0 + N], [],
                   [c for kc in range(q * 4, q * 4 + 4) for c in xT.cells(kc, 0, N)])

        def layer(l):
            base = l * NG
            dma_io(pbc.ap, pbc_d[l], [], pbc.cells())
            dma_io(pbs.ap, pbs_d[l], [], pbs.cells())
            dma_io(pbsS.ap, pbsS_d[l], [], pbsS.cells())
            rmsnorm_to_h(N, "gmix", l, stats_done=(l > 0))
            apply_norm_h(N, "gmix", l)

            for si, (c0, ln, src) in enumerate(segs):
                for (ext, eo, HL, hb, st_d) in ((a_ext, eA, HA, hAb, sta_d), (cgh, eB, HB, hBb, stb_d), (zd, eD, HD, hDb, std_d)):
                    o = eo[si]
                    wcells = [c for ch in range(4) for c in ext.cells(ch, o, o + HL)]
                    if src == "halo":
                        pass
                    elif src == "zero":
                        dve(lambda e, ext=ext, o=o, HL=HL: e.memset(ext.ap[:, :, o:o + HL], 0.0), [], wcells)
                    elif src == "carry":
                        act(lambda e, ext=ext, o=o, HL=HL, hb=hb: e.copy(out=ext.ap[:, :, o:o + HL], in_=hb.ap[:, l, :, :]),
                            hb.cells(), wcells)
                    else:
                        dma_io(ext.ap[:, :, o:o + HL], st_d[l, src], [], wcells)

            chk('%d.%d.p1' % (ti, l))
            def seg_evac(fn_seg, bank, ext, eo, HL, ch, eng, extra_r=()):
                for si, (c0, ln, src) in enumerate(segs):
                    o = eo[si] + HL
                    eng(lambda e, c0=c0, ln=ln, o=o: fn_seg(e, psum[:, bank, c0:c0 + ln], ext.ap[:, ch, o:o + ln], c0, ln),
                        psc(bank, c0, c0 + ln) + list(extra_r), ext.cells(ch, o, o + ln))
                if ti == 0:
                    dve(lambda e: e.tensor_scalar(out=ext.ap[:, ch, HL:HL + HALO], in0=ext.ap[:, ch, HL:HL + HALO],
                                                  scalar1=pcols.ap[:, PCO["flag"]:PCO["flag"] + 1], scalar2=None, op0=ALU.mult),
                        ext.cells(ch, HL, HL + HALO) + pcols.cells(), ext.cells(ch, HL, HL + HALO))

            bg = []

            def drain(n):
                for _ in range(min(n, len(bg))):
                    bg.pop(0)()

            s = load_group(base + GI[("in", 1)])
            proj_fm(s, N, lambda mi, bank: act(
                lambda e: e.activation(out=sgA.ap[:, mi, 0:N], in_=psum[:, bank, 0:N], func=AF.Sigmoid),
                psc(bank, 0, N), sgA.cells(mi, 0, N)))
            s = load_group(base + GI[("in", 0)])
            proj_fm(s, N, lambda mi, bank: seg_evac(
                lambda e, p_, o_, c0, ln: e.tensor_tensor(out=o_, in0=p_, in1=sgA.ap[:, mi, c0:c0 + ln], op=ALU.mult),
                bank, a_ext, eA, HA, mi, dve, sgA.cells(mi, 0, N)))
            PA = PB = PC = False
            engp = pool if (ti >= 1 and PC) else dve
            engb = pool if (ti >= 1 and PB) else dve
            for ch in range(4):
                for k in range(31):
                    wcol = pc("caw", l, ch * 31 + k)
                    if ti >= 1 and ch == 3 and PA:
                        if k == 0:
                            pool(lambda e, ch=ch, wcol=wcol: e.tensor_scalar(out=cacc.ap[:, ch, 0:CWa], in0=a_ext.ap[:, ch, 0:CWa],
                                                                             scalar1=wcol, scalar2=pc("cab", l, ch), op0=ALU.mult, op1=ALU.add),
                                 a_ext.cells(ch, 0, EWa) + pcols.cells() + sgA.cells(), cacc.cells(ch, 0, CWa))
                        else:
                            pool(lambda e, ch=ch, k=k, wcol=wcol: e.tensor_scalar(out=ptmp.ap[:, 0:CWa], in0=a_ext.ap[:, ch, k:k + CWa],
                                                                                  scalar1=wcol, scalar2=None, op0=ALU.mult),
                                 a_ext.cells(ch, 0, EWa) + pcols.cells(), ptmp.cells())
                            pool(lambda e, ch=ch: e.tensor_tensor(out=cacc.ap[:, ch, 0:CWa], in0=cacc.ap[:, ch, 0:CWa], in1=ptmp.ap[:, 0:CWa], op=ALU.add),
                                 cacc.cells(ch, 0, CWa) + ptmp.cells(), cacc.cells(ch, 0, CWa))
                        continue
                    if k == 0:
                        bg.append(lambda ch=ch, wcol=wcol: dve(
                            lambda e: e.tensor_scalar(out=cacc.ap[:, ch, 0:CWa], in0=a_ext.ap[:, ch, 0:CWa],
                                                      scalar1=wcol, scalar2=pc("cab", l, ch), op0=ALU.mult, op1=ALU.add),
                            a_ext.cells(ch, 0, EWa) + pcols.cells() + sgA.cells(), cacc.cells(ch, 0, CWa)))
                    else:
                        bg.append(lambda ch=ch, k=k, wcol=wcol: dve(
                            lambda e: e.scalar_tensor_tensor(
                                out=cacc.ap[:, ch, 0:CWa], in0=a_ext.ap[:, ch, k:k + CWa], scalar=wcol, in1=cacc.ap[:, ch, 0:CWa],
                                op0=ALU.mult, op1=ALU.add),
                            a_ext.cells(ch, 0, EWa) + cacc.cells(ch, 0, CWa) + pcols.cells(), cacc.cells(ch, 0, CWa)))
            drain(22)
            s = load_group(base + GI[("in", 3)])
            proj_fm(s, N, lambda mi, bank: act(
                lambda e: e.copy(out=cgt.ap[:, mi, 0:N], in_=psum[:, bank, 0:N]),
                psc(bank, 0, N), cgt.cells(mi, 0, N)))
            s = load_group(base + GI[("in", 4)])

            def evac_hb(mi, bank):
                seg_evac(lambda e, p_, o_, c0, ln: e.tensor_tensor(out=o_, in0=p_, in1=cgt.ap[:, mi, c0:c0 + ln], op=ALU.mult),
                         bank, cgh, eB, HB, mi, dve, cgt.cells(mi, 0, N))
                drain(5)
            proj_fm(s, N, evac_hb)
            for ch in range(4):
                for k in range(3):
                    wcol = pc("cbw", l, ch * 3 + k)
                    if k == 0:
                        dve(lambda e, ch=ch, wcol=wcol: e.tensor_scalar(out=bconv.ap[:, ch, 0:CWb], in0=cgh.ap[:, ch, 0:CWb],
                                                                        scalar1=wcol, scalar2=None, op0=ALU.mult),
                            cgh.cells(ch, 0, EWb) + pcols.cells(), bconv.cells(ch, 0, CWb))
                    else:
                        dve(lambda e, ch=ch, k=k, wcol=wcol: e.scalar_tensor_tensor(
                            out=bconv.ap[:, ch, 0:CWb], in0=cgh.ap[:, ch, k:k + CWb], scalar=wcol, in1=bconv.ap[:, ch, 0:CWb],
                            op0=ALU.mult, op1=ALU.add),
                            cgh.cells(ch, 0, EWb) + bconv.cells(ch, 0, CWb) + pcols.cells(), bconv.cells(ch, 0, CWb))
            s = load_group(base + GI[("in", 2)])

            def evac_bg(mi, bank):
                for si, (c0, ln, src) in enumerate(segs):
                    o = eB[si]
                    dve(lambda e, c0=c0, ln=ln, o=o: e.tensor_tensor(out=br.ap[:, 1, mi, c0:c0 + ln], in0=psum[:, bank, c0:c0 + ln],
                                                                     in1=bconv.ap[:, mi, o:o + ln], op=ALU.mult),
                        psc(bank, c0, c0 + ln) + bconv.cells(mi, 0, CWb), br.cells(1 * 4 + mi, c0, c0 + ln))
                drain(5)
            proj_fm(s, N, evac_bg)
            drain(22)
            s = load_group(base + GI[("in", 5)])
            proj_fm(s, N, lambda mi, bank: act(
                lambda e: e.activation(out=u.ap[:, mi, 0:N], in_=psum[:, bank, 0:N], func=AF.Gelu),
                psc(bank, 0, N), u.cells(mi, 0, N)))
            s = load_group(base + GI[("in", 6)])
            for ci, (c0, M, is_s) in enumerate(chunks):
                mm_group(4 + ci, 512,
                         [h.ap[:, kc, c0:c0 + M] for kc in range(KC)],
                         [wbuf[s].ap[:, kc, :] for kc in range(KC)],
                         [h.cells(kc, c0, c0 + M) for kc in range(KC)],
                         [wbuf[s].cells(kc) for kc in range(KC)], M=M)
                drain(10)
            drain(len(bg))

            def chain_step(ci, step):
                c0, M, is_s = chunks[ci]
                bank = 4 + ci
                vtb = vt[ci % 2]
                if step == 0:
                    act(lambda e: e.activation(out=vtb.ap[0:M, :], in_=psum[0:M, bank, :], func=AF.Gelu), psc(bank), vtb.cells())
                    for q in range(4):
                        dve(lambda e, q=q: e.bn_stats(out=st6.ap[0:M, q, :], in_=vtb.ap[0:M, q * 128:(q + 1) * 128]), vtb.cells(), st6.cells())
                    dve(lambda e: e.bn_aggr(out=mv.ap[0:M, :], in_=st6.ap[0:M, :, :]), st6.cells(), mv.cells())
                elif step == 1:
                    act(lambda e: e.activation(out=mv.ap[0:M, 1:2], in_=mv.ap[0:M, 1:2], func=AF.Sqrt, bias=epsL.ap[0:M, 0:1], scale=1.0),
                        mv.cells() + epsL.cells(), mv.cells())
                    dve(lambda e: e.reciprocal(out=mv.ap[0:M, 1:2], in_=mv.ap[0:M, 1:2]), mv.cells(), mv.cells())
                    dve(lambda e: e.tensor_scalar(out=vtb.ap[0:M, :], in0=vtb.ap[0:M, :], scalar1=mv.ap[0:M, 0:1], scalar2=mv.ap[0:M, 1:2],
                                                  op0=ALU.subtract, op1=ALU.mult), vtb.cells() + mv.cells(), vtb.cells())
                    dve(lambda e: e.tensor_tensor(out=vtb.ap[0:M, :], in0=vtb.ap[0:M, :], in1=pbc.ap[0:M, 0, :], op=ALU.mult),
                        vtb.cells() + pbc.cells(), vtb.cells())
                    dve(lambda e: e.tensor_tensor(out=vtb.ap[0:M, :], in0=vtb.ap[0:M, :], in1=pbc.ap[0:M, 1, :], op=ALU.add),
                        vtb.cells() + pbc.cells(), vtb.cells())
                elif step == 2:
                    act(lambda e: e.copy(out=vnbf.ap[0:M, ci, :], in_=vtb.ap[0:M, :]), vtb.cells(), vnbf.cells(ci))
                    if is_s:
                        dma_io(ovs[l], vtb.ap[0:NS, :], vtb.cells(), [])
                    elif ti == NTILE - 1 and ci == last_pc:
                        dma_io(ov[l], vtb.ap, vtb.cells(), [])

            def chain_hook(hi, mi):
                if hi < len(chunks) and mi < 3:
                    chain_step(hi, mi)

            def spatial(ci):
                c0, M, is_s = chunks[ci]
                for g in range(4):
                    rhs = WmS.ap[0:M, l, g, :] if is_s else WmT.ap[:, l, g, :]
                    pe(lambda e, g=g, rhs=rhs: e.matmul(psum[:, 7, g * 128:g * 128 + M], lhsT=vnbf.ap[0:M, ci, g * 128:(g + 1) * 128],
                                                        rhs=rhs, start=True, stop=True),
                       vnbf.cells(ci) + (WmS.cells() if is_s else WmT.cells()), psc(7, g * 128, g * 128 + M))
                bsrc = pbsS.ap if is_s else pbs.ap
                p7 = psum[:, 7, :].rearrange("p (g t) -> p g t", g=4)[:, :, 0:M]
                m3 = ex2.ap.rearrange("p (g t) -> p g t", g=4)[:, :, 0:M]
                dve(lambda e: e.tensor_tensor(out=m3, in0=p7, in1=bsrc, op=ALU.add),
                    psc(7) + pbs.cells() + pbsS.cells(), ex2.cells())
                dve(lambda e: e.tensor_tensor(out=br.ap[:, 2, :, c0:c0 + M], in0=m3, in1=u.ap[:, :, c0:c0 + M], op=ALU.mult),
                    ex2.cells() + u.cells(), [c for g in range(4) for c in br.cells(2 * 4 + g, c0, c0 + M)])

            def gate_group(b, mg, hook=None):
                s_ = load_group(base + GI[("gate", b, mg)])
                dst = (lambda mi: (sg.ap[:, b * 4 + mi, 0:N], sg.cells(b * 4 + mi, 0, N))) if b < 3 else \
                      (lambda mi: (sg3.ap[:, mi, 0:N], sg3.cells(mi, 0, N)))

                def ev(mi, bank):
                    o_, c_ = dst(mi)
                    act(lambda e: e.activation(out=o_, in_=psum[:, bank, 0:N], func=AF.Sigmoid), psc(bank, 0, N), c_)
                    if callable(hook):
                        hook(mi)
                    elif hook is not None:
                        chain_hook(hook, mi)
                proj_fm(s_, N, ev)

            def sg_of(b, mi):
                return (sg.ap[:, b * 4 + mi, 0:N], sg.cells(b * 4 + mi, 0, N)) if b < 3 else (sg3.ap[:, mi, 0:N], sg3.cells(mi, 0, N))

            s = load_group(base + GI[("in", 7)])
            def evac_zd(mi, bank):
                seg_evac(lambda e, p_, o_, c0, ln: e.copy(out=o_, in_=p_), bank, zd, eD, HD, mi, act)
                chain_hook(0, mi)
            proj_fm(s, N, evac_zd)
            for (ext, eo, HL, hb, o_s, o_p) in ((a_ext, eA, HA, hAb, oas, oa), (cgh, eB, HB, hBb, obs, ob), (zd, eD, HD, hDb, ods, od)):
                rc = ext.cells()
                for si_, sg_ in enumerate(segs):
                    if isinstance(sg_[2], int):
                        o = eo[si_] + 16
                        dma_io(o_s[l, sg_[2]], ext.ap[:, :, o:o + HL], rc, [])
                o = eo[i_pr] + segs[i_pr][1]
                act(lambda e, ext=ext, o=o, HL=HL, hb=hb: e.copy(out=hb.ap[:, l, :, :], in_=ext.ap[:, :, o:o + HL]), rc, hb.cells())
                if ti == NTILE - 1:
                    dma_io(o_p[l], hb.ap[:, l, :, :], hb.cells(), [])

            for g in range(4):
                src_b, bufs = zd, [a_ext, bconv]
                sh = 1
                for stp in range(g + 1):
                    dst_b = bufs[stp % 2]
                    engb(lambda e, g=g, sh=sh, src_b=src_b, dst_b=dst_b: e.tensor_tensor(
                        out=dst_b.ap[:, g, sh:EWd], in0=src_b.ap[:, g, sh:EWd], in1=src_b.ap[:, g, 0:EWd - sh], op=ALU.add),
                        src_b.cells(g, 0, EWd), dst_b.cells(g, 0, EWd))
                    src_b = dst_b
                    sh *= 2
                w = 2 ** (g + 1)
                if ti == 0:
                    oc_ = HD + HALO
                    dve(lambda e, g=g, src_b=src_b, oc_=oc_: e.tensor_tensor(out=src_b.ap[:, g, oc_:oc_ + 16], in0=src_b.ap[:, g, oc_:oc_ + 16],
                                                                     in1=pcorr.ap[:, g, :], op=ALU.mult),
                        src_b.cells(g, 0, EWd) + pcorr.cells(), src_b.cells(g, 0, EWd))
                for si, (c0, ln, src) in enumerate(segs):
                    o = eD[si] + HD
                    dve(lambda e, g=g, w=w, src_b=src_b, c0=c0, ln=ln, o=o: e.scalar_tensor_tensor(
                        out=pooled.ap[:, g, c0:c0 + ln], in0=src_b.ap[:, g, o:o + ln], scalar=1.0 / w, in1=zd.ap[:, g, o:o + ln],
                        op0=ALU.mult, op1=ALU.subtract),
                        src_b.cells(g, 0, EWd) + zd.cells(g, 0, EWd), pooled.cells(g, c0, c0 + ln))
            chk('%d.%d.p2' % (ti, l))
            nch = len(chunks)
            gate_group(0, 0, hook=1)
            for g in range(4):
                bank = nb_sec()
                pe(lambda e, g=g, bank=bank: e.matmul(psum[:, bank, 0:N], lhsT=pwb.ap[:, l, g, :], rhs=pooled.ap[:, g, 0:N], start=True, stop=True),
                   pwb.cells() + pooled.cells(g, 0, N), psc(bank, 0, N))
                act(lambda e, g=g, bank=bank: e.activation(out=br.ap[:, 3, g, 0:N], in_=psum[:, bank, 0:N], func=AF.Copy, scale=pc("psc", l, g)),
                    psc(bank, 0, N) + pcols.cells(), br.cells(3 * 4 + g, 0, N))
            gate_group(1, 0, hook=2)
            gate_group(2, 0, hook=3)
            for ci in range(nch):
                spatial(ci)

            for ch in range(4):
                act(lambda e, ch=ch: e.copy(out=cbf.ap[:, ch, 0:CWa], in_=cacc.ap[:, ch, 0:CWa]), cacc.cells(ch, 0, CWa), cbf.cells(ch, 0, CWa))
                act(lambda e, ch=ch: e.activation(out=sq.ap[:, ch, 0:CWa], in_=cacc.ap[:, ch, 0:CWa], func=AF.Square),
                    cacc.cells(ch, 0, CWa), sq.cells(ch, 0, CWa))
            for ch in range(4):
                pe(lambda e, ch=ch: e.matmul(psum[:, 6, 0:CWa], lhsT=onesA.ap, rhs=cbf.ap[:, ch, 0:CWa], start=(ch == 0), stop=(ch == 3)),
                   onesA.cells() + cbf.cells(ch, 0, CWa), psc(6, 0, CWa))
            for ch in range(4):
                pe(lambda e, ch=ch: e.matmul(psum[:, 7, 0:CWa], lhsT=onesA.ap, rhs=sq.ap[:, ch, 0:CWa], start=(ch == 0), stop=(ch == 3)),
                   onesA.cells() + sq.cells(ch, 0, CWa), psc(7, 0, CWa))
            def ln_hook(mi):
                if mi == 0:
                    act(lambda e: e.copy(out=mean.ap[:, 0:CWa], in_=psum[:, 6, 0:CWa]), psc(6, 0, CWa), mean.cells())
                    dve(lambda e: e.tensor_tensor(out=ex2.ap[:, 0:CWa], in0=mean.ap[:, 0:CWa], in1=mean.ap[:, 0:CWa], op=ALU.mult), mean.cells(), ex2.cells())
                    dve(lambda e: e.tensor_tensor(out=ex2.ap[:, 0:CWa], in0=psum[:, 7, 0:CWa], in1=ex2.ap[:, 0:CWa], op=ALU.subtract),
                        psc(7, 0, CWa) + ex2.cells(), ex2.cells())
                    dve(lambda e: e.tensor_scalar(out=ex2.ap[:, 0:CWa], in0=ex2.ap[:, 0:CWa], scalar1=0.0, scalar2=None, op0=ALU.max), ex2.cells(), ex2.cells())
                elif mi == 1:
                    act(lambda e: e.activation(out=rstdA.ap[:, 0:CWa], in_=ex2.ap[:, 0:CWa], func=AF.Sqrt, bias=epsL.ap[:, 0:1], scale=1.0),
                        ex2.cells() + epsL.cells(), rstdA.cells())
                    dve(lambda e: e.reciprocal(out=rstdA.ap[:, 0:CWa], in_=rstdA.ap[:, 0:CWa]), rstdA.cells(), rstdA.cells())
                    for ch in range(4):
                        dve(lambda e, ch=ch: e.tensor_tensor(out=cacc.ap[:, ch, 0:CWa], in0=cacc.ap[:, ch, 0:CWa], in1=mean.ap[:, 0:CWa], op=ALU.subtract),
                            cacc.cells(ch, 0, CWa) + mean.cells(), cacc.cells(ch, 0, CWa))
                        dve(lambda e, ch=ch: e.tensor_tensor(out=cacc.ap[:, ch, 0:CWa], in0=cacc.ap[:, ch, 0:CWa], in1=rstdA.ap[:, 0:CWa], op=ALU.mult),
                            cacc.cells(ch, 0, CWa) + rstdA.cells(), cacc.cells(ch, 0, CWa))
                else:
                    for ch in ((0, 1) if mi == 2 else (2, 3)):
                        for si, (c0, ln, src) in enumerate(segs):
                            o = eA[si]
                            act(lambda e, ch=ch, c0=c0, ln=ln, o=o: e.activation(out=br.ap[:, 0, ch, c0:c0 + ln], in_=cacc.ap[:, ch, o:o + ln], func=AF.Silu,
                                                                                 scale=pc("lag", l, ch), bias=pc("lab", l, ch)),
                                cacc.cells(ch, 0, CWa) + pcols.cells(), br.cells(0 * 4 + ch, c0, c0 + ln))
            gate_group(3, 0, hook=ln_hook)
            chk('%d.%d.p3' % (ti, l))
            for mg in range(4):
                if mg > 0:
                    for b in range(4):
                        gate_group(b, mg)
                s = load_group(base + GI[("wo", mg)])
                BORD = (1, 2, 3, 0)
                for bi, b in enumerate(BORD):
                    for mi in range(4):
                        m = mg * 4 + mi
                        bank = 4 + wo_rr[0] % 4
                        wo_rr[0] += 1
                        mm_group(bank, N,
                                 [wbuf[s].ap[:, b * 4 + kc, mi * 128:(mi + 1) * 128] for kc in range(4)],
                                 [br.ap[:, b, kc, 0:N] for kc in range(4)],
                                 [wbuf[s].cells(b * 4 + kc) for kc in range(4)],
                                 [br.cells(b * 4 + kc, 0, N) for kc in range(4)])
                        sga, sgc = sg_of(b, mi)
                        if bi == 0:
                            dve(lambda e, mi=mi, bank=bank, sga=sga: e.tensor_tensor(out=macc.ap[:, mi, 0:N], in0=psum[:, bank, 0:N],
                                                                                    in1=sga, op=ALU.mult),
                                psc(bank, 0, N) + sgc, macc.cells(mi, 0, N))
                        else:
                            tb = (bi * 4 + mi) % 2
                            dve(lambda e, mi=mi, tb=tb, bank=bank, sga=sga: e.tensor_tensor(out=mtmp.ap[:, tb, 0:N], in0=psum[:, bank, 0:N],
                                                                                           in1=sga, op=ALU.mult),
                                psc(bank, 0, N) + sgc, mtmp.cells(tb, 0, N))
                            if bi < 3:
                                dve(lambda e, mi=mi, tb=tb: e.tensor_tensor(out=macc.ap[:, mi, 0:N], in0=macc.ap[:, mi, 0:N], in1=mtmp.ap[:, tb, 0:N], op=ALU.add),
                                    macc.cells(mi, 0, N) + mtmp.cells(tb, 0, N), macc.cells(mi, 0, N))
                            else:
                                dve(lambda e, mi=mi, m=m, tb=tb: e.tensor_tensor(out=mgb.ap[:, m, 0:N], in0=macc.ap[:, mi, 0:N], in1=mtmp.ap[:, tb, 0:N], op=ALU.add),
                                    macc.cells(mi, 0, N) + mtmp.cells(tb, 0, N), mgb.cells(m, 0, N))
            chk('%d.%d.p4' % (ti, l))
            for og in range(4):
                s = load_group(base + GI[("o", og)])
                for mi in range(4):
                    m = og * 4 + mi
                    bank = nb_main()
                    mm_group(bank, N,
                             [wbuf[s].ap[:, kc, mi * 128:(mi + 1) * 128] for kc in range(KC)],
                             [mgb.ap[:, kc, 0:N] for kc in range(KC)],
                             [wbuf[s].cells(kc) for kc in range(KC)],
                             [mgb.cells(kc, 0, N) for kc in range(KC)])
                    dve(lambda e, m=m, bank=bank: e.tensor_tensor(out=xT.ap[:, m, 0:N], in0=xT.ap[:, m, 0:N], in1=psum[:, bank, 0:N], op=ALU.add),
                        xT.cells(m, 0, N) + psc(bank, 0, N), xT.cells(m, 0, N))
                    if m >= 2:
                        sq_stat(N, m - 2)
            sq_stat(N, KC - 2)
            sq_stat(N, KC - 1)
            chk('%d.%d.p5' % (ti, l))
            rmsnorm_to_h(N, "gffn", l, stats_done=True)
            apply_norm_h(N, "gffn", l)
            chk('%d.%d.p6' % (ti, l))
            for fg in range(11):
                s1 = load_group(base + GI[("w1", fg)])
                s3 = load_group(base + GI[("w3", fg)])
                for fi in range(4):
                    f = fg * 4 + fi
                    bA = nb_main()
                    mm_group(bA, N,
                             [wbuf[s1].ap[:, kc, fi * 128:(fi + 1) * 128] for kc in range(KC)],
                             [h.ap[:, kc, 0:N] for kc in range(KC)],
                             [wbuf[s1].cells(kc) for kc in range(KC)], [h.cells(kc, 0, N) for kc in range(KC)])
                    act(lambda e, f=f, bA=bA: e.activation(out=ftmp.ap[:, f % 2, 0:N], in_=psum[:, bA, 0:N], func=AF.Silu),
                        psc(bA, 0, N), ftmp.cells(f % 2, 0, N))
                    bB = nb_sec()
                    mm_group(bB, N,
                             [wbuf[s3].ap[:, kc, fi * 128:(fi + 1) * 128] for kc in range(KC)],
                             [h.ap[:, kc, 0:N] for kc in range(KC)],
                             [wbuf[s3].cells(kc) for kc in range(KC)], [h.cells(kc, 0, N) for kc in range(KC)])
                    dve(lambda e, f=f, bB=bB: e.tensor_tensor(out=hid.ap[:, f, 0:N], in0=psum[:, bB, 0:N], in1=ftmp.ap[:, f % 2, 0:N], op=ALU.mult),
                        psc(bB, 0, N) + ftmp.cells(f % 2, 0, N), hid.cells(f, 0, N))
            for og in range(4):
                for kg in range(4):
                    if kg == 1 and og > 0:
                        for m in range((og - 1) * 4, og * 4):
                            sq_stat(N, m)
                    s = load_group(base + GI[("w2", og, kg)])
                    for mi in range(4):
                        for j in range(11):
                            f = kg * 11 + j
                            pe(lambda e, mi=mi, j=j, f=f, s=s, kg=kg: e.matmul(psum[:, mi, 0:N], lhsT=wbuf[s].ap[:, j, mi * 128:(mi + 1) * 128],
                                                                              rhs=hid.ap[:, f, 0:N], start=(kg == 0 and j == 0), stop=(kg == 3 and j == 10)),
                               wbuf[s].cells(j) + hid.cells(f, 0, N), psc(mi, 0, N))
                for mi in range(4):
                    m = og * 4 + mi
                    dve(lambda e, m=m, mi=mi: e.tensor_tensor(out=xT.ap[:, m, 0:N], in0=xT.ap[:, m, 0:N], in1=psum[:, mi, 0:N], op=ALU.add),
                        xT.cells(m, 0, N) + psc(mi, 0, N), xT.cells(m, 0, N))
                bank_main[0] = 0
            for m in range(12, KC):
                sq_stat(N, m)
        for l_ in range(L):
            layer(l_)
        chk('%d.p7' % ti)
        rmsnorm_to_h(N, "gfin", None, stats_done=True)
        for q in range(4):
            for j in range(4):
                kc = q * 4 + j
                gcol = pc("gfin", None, kc)
                dve(lambda e, kc=kc, j=j, q=q, gcol=gcol: e.scalar_tensor_tensor(out=ystg.ap[:, q % 2, j, 0:N], in0=xT.ap[:, kc, 0:N], scalar=gcol,
                                                                                 in1=rs.ap[:, 0:N], op0=ALU.mult, op1=ALU.mult),
                    xT.cells(kc, 0, N) + rs.cells() + pcols.cells(), sbc(ystg.off + (q % 2) * 4 * NT * 4 + j * NT * 4, N * 4))
            dma_io(yT[:, q * 4:(q + 1) * 4, tok0:tok0 + N], ystg.ap[:, q % 2, :, 0:N], sbc(ystg.off + (q % 2) * 4 * NT * 4, 4 * NT * 4), [])

    for ti in range(NTILE):
        tile(ti)
        chk('%d.end' % ti)

    cnt = {e: 0 for e in sem_eng}
    dcum = {}
    dlast = {}
    for I in P.ins:
        if I.dma:
            k = id(I.dsem)
            I.dprev = dcum.get(k, 0)
            I.dval = I.dprev + 16
            dcum[k] = I.dval
        elif I.signal:
            cnt[I.eng] += 1
            I.count = cnt[I.eng]

    streams = {e: [] for e in ("pe", "act", "dve", "pool", "sp")}
    for I in P.ins:
        streams[I.eng].append(I)
    final_io = [(s, dcum.get(id(s), 0)) for s in sem_io]

    def emit(engname, e):
        waited = {}

        def wait(sem, val):
            k = id(sem)
            if waited.get(k, 0) >= val:
                return
            waited[k] = val
            e.wait_ge(sem, val)
        for I in streams[engname]:
            need = {}
            for d in I.deps:
                sm, v = (d.dsem, d.dval) if d.dma else (sem_eng[d.eng], d.count)
                if need.get(id(sm), (None, 0))[1] < v:
                    need[id(sm)] = (sm, v)
            for (sm, v) in need.values():
                wait(sm, v)
            if I.dma:
                if I.dprev:
                    wait(I.dsem, I.dprev)
                I.fn(e).then_inc(I.dsem, 16)
            else:
                bi = I.fn(e)
                if I.signal:
                    bi.then_inc(sem_eng[I.eng], 1)
        if engname == "act":
            for (s, v) in final_io:
                if v:
                    wait(s, v)

    with nc.Block() as block:
        @block.tensor
        def _(e):
            emit("pe", e)

        @block.scalar
        def _(e):
            emit("act", e)

        @block.vector
        def _(e):
            emit("dve", e)

        @block.gpsimd
        def _(e):
            emit("pool", e)

        @block.sync
        def _(e):
            emit("sp", e)
    es.close()
    return nc


def _rep(a):
    return np.ascontiguousarray(np.broadcast_to(a[None], (128,) + a.shape))


def kernel(x_prompt, x_sample, state_conv_a, state_conv_b, state_pool, norm_mix_g, w_in, conv_a_w,
           conv_a_b, ln_a_g, ln_a_b, w_out_a, conv_b_w, w_out_b, ln_c_g, ln_c_b, spatial_w, spatial_b,
           w_out_c, pool_w, pool_scale, w_out_d, w_o, norm_ffn_g, ffn_w1, ffn_w3, ffn_w2, norm_final_g):
    f = lambda a: np.ascontiguousarray(np.asarray(a, dtype=np.float32))
    x_prompt, x_sample = f(x_prompt), f(x_sample)
    col = lambda v, n: np.asarray(v, np.float32).reshape(n, 128).T
    shared = {
        "w_in": f(w_in), "w_out_a": f(w_out_a), "w_out_b": f(w_out_b), "w_out_c": f(w_out_c), "w_out_d": f(w_out_d),
        "w_o": f(w_o), "ffn_w1": f(ffn_w1), "ffn_w3": f(ffn_w3), "ffn_w2": f(ffn_w2),
    }
    pcb = np.zeros((128, NPC), np.float32)
    for l in range(L):
        pcb[:, PCO[("gmix", l)]:PCO[("gmix", l)] + 16] = col(norm_mix_g[l], 16)
        pcb[:, PCO[("gffn", l)]:PCO[("gffn", l)] + 16] = col(norm_ffn_g[l], 16)
        caw = np.asarray(conv_a_w[l], np.float32)
        pcb[:, PCO[("caw", l)]:PCO[("caw", l)] + 124] = caw.reshape(31, 4, 128).transpose(2, 1, 0).reshape(128, 124)
        pcb[:, PCO[("cab", l)]:PCO[("cab", l)] + 4] = col(conv_a_b[l], 4)
        pcb[:, PCO[("lag", l)]:PCO[("lag", l)] + 4] = col(ln_a_g[l], 4)
        pcb[:, PCO[("lab", l)]:PCO[("lab", l)] + 4] = col(ln_a_b[l], 4)
        cbw = np.asarray(conv_b_w[l], np.float32)
        pcb[:, PCO[("cbw", l)]:PCO[("cbw", l)] + 12] = cbw.reshape(3, 4, 128).transpose(2, 1, 0).reshape(128, 12)
        pcb[:, PCO[("psc", l)]:PCO[("psc", l)] + 4] = col(pool_scale[l], 4)
    pcb[:, PCO["gfin"]:PCO["gfin"] + 16] = col(norm_final_g, 16)
    pbc = np.stack([np.stack([_rep(f(ln_c_g)[l]), _rep(f(ln_c_b)[l])], axis=1) for l in range(L)])
    sb = f(spatial_b)
    pbs = np.stack([_rep(sb[l]) for l in range(L)])
    pbsS = np.stack([_rep(np.concatenate([sb[l][:, :16], sb[l][:, :16]], axis=1)) for l in range(L)])
    sw = f(spatial_w)
    wsT = np.ascontiguousarray(sw.transpose(3, 0, 1, 2))
    wsS = np.zeros((32, L, 4, 32), np.float32)
    for q in range(2):
        wsS[q * 16:(q + 1) * 16, :, :, q * 16:(q + 1) * 16] = wsT[:16, :, :, :16]
    maskP = np.triu(np.ones((128, 128), np.float32))
    maskS = np.zeros((32, 32), np.float32)
    for q in range(2):
        maskS[q * 16:(q + 1) * 16, q * 16:(q + 1) * 16] = np.triu(np.ones((16, 16), np.float32))
    pw = np.ascontiguousarray(f(pool_w).transpose(2, 0, 1, 3))

    def st_layout(s, HL):
        s = f(s).reshape(L, 16, HL, 4, 128).transpose(0, 1, 4, 3, 2)
        return np.ascontiguousarray(s)
    sta_all, stb_all, std_all = st_layout(state_conv_a, HA), st_layout(state_conv_b, HB), st_layout(state_pool, HD)

    in_maps = []
    for c in range(NCORE):
        b, sgi = c // 4, c % 4
        toks = np.zeros((NTOK, D), np.float32)
        if sgi > 0:
            toks[0:HALO] = x_prompt[b, sgi * SEG - HALO:sgi * SEG]
        toks[PEND:PEND + 16] = x_sample[2 * c]
        toks[PEND + 16:PEND + 32] = x_sample[2 * c + 1]
        toks[HALO:PEND] = x_prompt[b, sgi * SEG:(sgi + 1) * SEG]
        xin = np.ascontiguousarray(toks.reshape(NTOK, KC, 128).transpose(2, 1, 0))
        pc_c = pcb.copy()
        pc_c[:, PCO["flag"]] = 0.0 if sgi == 0 else 1.0
        pcorr = np.ones((128, 4, 16), np.float32)
        if sgi == 0:
            for g, w in enumerate((2, 4, 8, 16)):
                for t in range(16):
                    pcorr[:, g, t] = np.float32(w) / np.float32(min(t + 1, w))
        m = dict(shared)
        m.update({
            "xin": xin, "pcols": pc_c, "pbc": pbc, "pbs": pbs, "pbsS": pbsS, "wsT": wsT, "wsS": wsS,
            "maskP": maskP, "maskS": maskS, "pw": pw, "pcorr": pcorr,
            "sta": np.ascontiguousarray(sta_all[:, 2 * c:2 * c + 2]),
            "stb": np.ascontiguousarray(stb_all[:, 2 * c:2 * c + 2]),
            "std": np.ascontiguousarray(std_all[:, 2 * c:2 * c + 2]),
        })
        in_maps.append(m)

    nc = build_nc()
    res = run_bass_kernel_spmd(nc, in_maps, core_ids=list(range(NCORE)))
    R = list(res.results)

    y_prompt = np.zeros((2, 4 * SEG, D), np.float32)
    y_sample = np.zeros((16, 16, D), np.float32)
    for c in range(NCORE):
        b, sgi = c // 4, c % 4
        yt = np.asarray(R[c]["yT"]).transpose(2, 1, 0).reshape(NTOK, D)
        y_prompt[b, sgi * SEG:(sgi + 1) * SEG] = yt[HALO:PEND]
        y_sample[2 * c] = yt[PEND:PEND + 16]
        y_sample[2 * c + 1] = yt[PEND + 16:PEND + 32]

    def st_back(a):
        return np.asarray(a).transpose(2, 1, 0).reshape(a.shape[2], 512)
    outs = []
    for (name_p, name_s, HL) in (("oa", "oas", HA), ("ob", "obs", HB), ("od", "ods", HD)):
        p = np.zeros((L, 2, HL, 512), np.float32)
        s = np.zeros((L, 16, HL, 512), np.float32)
        for l in range(L):
            for b in range(2):
                p[l, b] = st_back(R[b * 4 + 3][name_p][l])
            for c in range(NCORE):
                for q in range(2):
                    s[l, 2 * c + q] = st_back(R[c][name_s][l, q])
        outs += [p, s]
    vp = np.zeros((L, 2, 128, 512), np.float32)
    vs = np.zeros((L, 16, 16, 512), np.float32)
    for l in range(L):
        for b in range(2):
            vp[l, b] = np.asarray(R[b * 4 + 3]["ov"][l])
        for c in range(NCORE):
            o = np.asarray(R[c]["ovs"][l])
            vs[l, 2 * c] = o[0:16]
            vs[l, 2 * c + 1] = o[16:32]
    return (y_prompt, y_sample, outs[0], outs[1], outs[2], outs[3], outs[4], outs[5], vp, vs)
```
